# Optimizing a Trainium2 kernel written in Bass

```python
import math
import jax, jax.numpy as jnp
from jax import lax
import numpy as np

D_MODEL = 1024
BATCH = 16
SEQ = 2048
DEPTH = 2

N_MIXERS = 2
N_A_LAYERS = (DEPTH + N_MIXERS - 1) // N_MIXERS
N_B_LAYERS = DEPTH // N_MIXERS

ATTN_HEAD_DIM = 128
ATTN_HEADS = D_MODEL // ATTN_HEAD_DIM
MOBA_BLOCK = 256
MOBA_TOPK = 3
Q_CHUNK = 128
ROPE_THETA = 10000.0

SSM_EXPAND = 2
SSM_D_INNER = SSM_EXPAND * D_MODEL
SSM_HEAD_DIM = 64
SSM_HEADS = SSM_D_INNER // SSM_HEAD_DIM
SSM_GROUPS = 8
SSM_HEADS_PER_GROUP = SSM_HEADS // SSM_GROUPS
SSM_STATE = 128
SSM_CONV = 4
SSM_CHUNK = 128
SSM_CONV_DIM = SSM_D_INNER + 2 * SSM_GROUPS * SSM_STATE
SSM_IN_DIM = SSM_D_INNER + SSM_CONV_DIM + SSM_HEADS

D_FF = 2816
EPS = 1e-6

kernel_name = "moba_mamba2_macaron_hybrid"


def rms_norm(x, g):
    xf = x.astype(jnp.float32)
    y = xf * lax.rsqrt(jnp.mean(xf * xf, axis=-1, keepdims=True) + EPS)
    return (y * g.astype(jnp.float32)).astype(x.dtype)


def swiglu(x, w_gate, w_up, w_down):
    return (jax.nn.silu(x @ w_gate) * (x @ w_up)) @ w_down


def rope(x, positions):
    half = x.shape[-1] // 2
    inv_freq = ROPE_THETA ** (-jnp.arange(half, dtype=jnp.float32) / half)
    ang = positions.astype(jnp.float32)[..., None] * inv_freq
    cos = jnp.cos(ang)[:, :, None, :]
    sin = jnp.sin(ang)[:, :, None, :]
    xf = x.astype(jnp.float32)
    x1, x2 = xf[..., :half], xf[..., half:]
    out = jnp.concatenate([x1 * cos - x2 * sin, x2 * cos + x1 * sin], axis=-1)
    return out.astype(x.dtype)


def moba_mixer(h, positions, w_qkv, q_gain, k_gain, w_o):
    bsz, seq, _ = h.shape
    H, Dh, BS = ATTN_HEADS, ATTN_HEAD_DIM, MOBA_BLOCK
    qkv = (h @ w_qkv).reshape(bsz, seq, 3, H, Dh)
    q = rope(rms_norm(qkv[:, :, 0], q_gain), positions)
    k = rope(rms_norm(qkv[:, :, 1], k_gain), positions)
    v = qkv[:, :, 2]
    q = jnp.swapaxes(q, 1, 2)
    k = jnp.swapaxes(k, 1, 2)
    v = jnp.swapaxes(v, 1, 2)
    n_blk = -(-seq // BS)
    pad = n_blk * BS - seq
    k = jnp.pad(k, ((0, 0), (0, 0), (0, pad), (0, 0)))
    v = jnp.pad(v, ((0, 0), (0, 0), (0, pad), (0, 0)))
    kb = k.reshape(bsz, H, n_blk, BS, Dh)
    vb = v.reshape(bsz, H, n_blk, BS, Dh)
    n_sel = min(MOBA_TOPK, n_blk - 1)
    q_blk = jnp.arange(seq) // BS
    scale = Dh ** -0.5

    if n_sel > 0:
        k_mean = jnp.mean(kb.astype(jnp.float32), axis=3)
        gate = jnp.einsum('bhtd,bhnd->bhtn', q.astype(jnp.float32), k_mean)
        past = jnp.arange(n_blk)[None, :] < q_blk[:, None]
        gate = jnp.where(past, gate, -jnp.inf)
        _, sel_idx = lax.top_k(gate, n_sel)
        sel_valid = sel_idx < q_blk[:, None]

    n_qc = seq // Q_CHUNK
    head_ix = jnp.arange(H)[:, None, None]

    def one_chunk(s):
        b = s // n_qc
        t0 = (s % n_qc) * Q_CHUNK
        qc = lax.dynamic_slice_in_dim(lax.dynamic_index_in_dim(q, b, 0, False), t0, Q_CHUNK, 1)
        kb_b = lax.dynamic_index_in_dim(kb, b, 0, False)
        vb_b = lax.dynamic_index_in_dim(vb, b, 0, False)
        own = t0 // BS
        k_own = lax.dynamic_index_in_dim(kb_b, own, 1, False)
        v_own = lax.dynamic_index_in_dim(vb_b, own, 1, False)
        q_pos = t0 + jnp.arange(Q_CHUNK)
        k_pos = own * BS + jnp.arange(BS)
        s_own = jnp.einsum('hqd,hkd->hqk', qc, k_own).astype(jnp.float32) * scale
        s_own = jnp.where(k_pos[None, :] <= q_pos[:, None], s_own, -jnp.inf)
        if n_sel == 0:
            p = jax.nn.softmax(s_own, axis=-1).astype(v_own.dtype)
            return jnp.einsum('hqk,hkd->hqd', p, v_own)
        idx = lax.dynamic_slice_in_dim(lax.dynamic_index_in_dim(sel_idx, b, 0, False), t0, Q_CHUNK, 1)
        valid = lax.dynamic_slice_in_dim(lax.dynamic_index_in_dim(sel_valid, b, 0, False), t0, Q_CHUNK, 1)
        k_sel = kb_b[head_ix, idx]
        v_sel = vb_b[head_ix, idx]
        s_sel = jnp.einsum('hqd,hqjkd->hqjk', qc, k_sel).astype(jnp.float32) * scale
        s_sel = jnp.where(valid[..., None], s_sel, -jnp.inf).reshape(H, Q_CHUNK, n_sel * BS)
        p = jax.nn.softmax(jnp.concatenate([s_sel, s_own], axis=-1), axis=-1).astype(v_own.dtype)
        p_sel = p[..., :n_sel * BS].reshape(H, Q_CHUNK, n_sel, BS)
        p_own = p[..., n_sel * BS:]
        return (jnp.einsum('hqjk,hqjkd->hqd', p_sel, v_sel)
                + jnp.einsum('hqk,hkd->hqd', p_own, v_own))

    o = lax.map(one_chunk, jnp.arange(bsz * n_qc))
    o = o.reshape(bsz, n_qc, H, Q_CHUNK, Dh).transpose(0, 1, 3, 2, 4).reshape(bsz, seq, H * Dh)
    return o @ w_o


def ssd_chunked(xdt, a_dt, bm, cm):
    bsz, seq = xdt.shape[:2]
    G, R, P, N, CH = SSM_GROUPS, SSM_HEADS_PER_GROUP, SSM_HEAD_DIM, SSM_STATE, SSM_CHUNK
    nc = seq // CH
    x = xdt.reshape(bsz, nc, CH, G, R, P)
    a = a_dt.reshape(bsz, nc, CH, G, R).transpose(0, 3, 4, 1, 2)
    bc = bm.reshape(bsz, nc, CH, G, N)
    cc = cm.reshape(bsz, nc, CH, G, N)
    a_cs = jnp.cumsum(a, axis=-1)
    tril = jnp.tril(jnp.ones((CH, CH), dtype=bool))
    seg = a_cs[..., :, None] - a_cs[..., None, :]
    L = jnp.exp(jnp.where(tril, seg, -jnp.inf))
    cb = jnp.einsum('bclgn,bcsgn->bgcls', cc, bc)
    y_diag = jnp.einsum('bgrcls,bcsgrp->bclgrp', cb[:, :, None] * L, x)
    decay = jnp.exp(a_cs[..., -1:] - a_cs)
    states = jnp.einsum('bclgn,bgrcl,bclgrp->bcgrpn', bc, decay, x)
    chunk_decay = jnp.exp(a_cs[..., -1])

    def step(hs, inp):
        st, dec = inp
        return hs * dec[..., None, None] + st, hs

    h0 = jnp.zeros((bsz, G, R, P, N), dtype=jnp.float32)
    _, prev = lax.scan(step, h0, (jnp.moveaxis(states, 1, 0), jnp.moveaxis(chunk_decay, -1, 0)))
    prev = jnp.moveaxis(prev, 0, 1)
    y_off = jnp.einsum('bclgn,bcgrpn,bgrcl->bclgrp', cc, prev, jnp.exp(a_cs))
    return (y_diag + y_off).reshape(bsz, seq, G * R, P)


def mamba2_mixer(h, w_in, conv_w, conv_b, dt_bias, a_log, d_skip, norm_g, w_out):
    bsz, seq, _ = h.shape
    DI, G, N, H, P = SSM_D_INNER, SSM_GROUPS, SSM_STATE, SSM_HEADS, SSM_HEAD_DIM
    proj = h @ w_in
    z = proj[..., :DI]
    xbc = proj[..., DI:DI + SSM_CONV_DIM]
    dt = proj[..., DI + SSM_CONV_DIM:]
    xbc = lax.conv_general_dilated(xbc, conv_w[:, None, :], window_strides=(1,),
                                   padding=[(SSM_CONV - 1, 0)],
                                   dimension_numbers=('NWC', 'WIO', 'NWC'),
                                   feature_group_count=SSM_CONV_DIM)
    xbc = jax.nn.silu(xbc + conv_b)
    xs = xbc[..., :DI].reshape(bsz, seq, H, P).astype(jnp.float32)
    bm = xbc[..., DI:DI + G * N].reshape(bsz, seq, G, N).astype(jnp.float32)
    cm = xbc[..., DI + G * N:].reshape(bsz, seq, G, N).astype(jnp.float32)
    dt = jax.nn.softplus(dt.astype(jnp.float32) + dt_bias.astype(jnp.float32))
    a = -jnp.exp(a_log.astype(jnp.float32))
    y = ssd_chunked(xs * dt[..., None], dt * a, bm, cm)
    y = y + d_skip.astype(jnp.float32)[:, None] * xs
    y = y.reshape(bsz, seq, DI) * jax.nn.silu(z.astype(jnp.float32))
    yg = y.reshape(bsz, seq, G, DI // G)
    yg = yg * lax.rsqrt(jnp.mean(yg * yg, axis=-1, keepdims=True) + EPS)
    y = yg.reshape(bsz, seq, DI) * norm_g.astype(jnp.float32)
    return y.astype(h.dtype) @ w_out


def setup_inputs(seed: int = 0) -> dict:
    key = jax.random.key(seed)
    ks = jax.random.split(key, 26)
    f32 = jnp.float32

    def nrm(k, shape, fan_in):
        return jax.random.normal(k, shape, f32) * (fan_in ** -0.5)

    def gain(k, shape):
        return 1.0 + 0.05 * jax.random.normal(k, shape, f32)

    dt_init = jnp.exp(jax.random.uniform(ks[20], (N_B_LAYERS, SSM_HEADS), f32)
                      * (math.log(0.1) - math.log(0.001)) + math.log(0.001))
    dt_init = jnp.maximum(dt_init, 1e-4)
    return {
        "x": jax.random.normal(ks[0], (BATCH, SEQ, D_MODEL), f32),
        "positions": jnp.broadcast_to(jnp.arange(SEQ, dtype=jnp.int32), (BATCH, SEQ)),
        "ffn1_norm": gain(ks[1], (DEPTH, D_MODEL)),
        "ffn1_w_gate": nrm(ks[2], (DEPTH, D_MODEL, D_FF), D_MODEL),
        "ffn1_w_up": nrm(ks[3], (DEPTH, D_MODEL, D_FF), D_MODEL),
        "ffn1_w_down": nrm(ks[4], (DEPTH, D_FF, D_MODEL), D_FF),
        "mix_norm": gain(ks[5], (DEPTH, D_MODEL)),
        "ffn2_norm": gain(ks[6], (DEPTH, D_MODEL)),
        "ffn2_w_gate": nrm(ks[7], (DEPTH, D_MODEL, D_FF), D_MODEL),
        "ffn2_w_up": nrm(ks[8], (DEPTH, D_MODEL, D_FF), D_MODEL),
        "ffn2_w_down": nrm(ks[9], (DEPTH, D_FF, D_MODEL), D_FF),
        "attn_w_qkv": nrm(ks[10], (N_A_LAYERS, D_MODEL, 3 * D_MODEL), D_MODEL),
        "attn_q_norm": gain(ks[11], (N_A_LAYERS, ATTN_HEAD_DIM)),
        "attn_k_norm": gain(ks[12], (N_A_LAYERS, ATTN_HEAD_DIM)),
        "attn_w_o": nrm(ks[13], (N_A_LAYERS, D_MODEL, D_MODEL), D_MODEL),
        "ssm_w_in": nrm(ks[14], (N_B_LAYERS, D_MODEL, SSM_IN_DIM), D_MODEL),
        "ssm_conv_w": nrm(ks[15], (N_B_LAYERS, SSM_CONV, SSM_CONV_DIM), SSM_CONV),
        "ssm_conv_b": 0.02 * jax.random.normal(ks[16], (N_B_LAYERS, SSM_CONV_DIM), f32),
        "ssm_dt_bias": dt_init + jnp.log(-jnp.expm1(-dt_init)),
        "ssm_a_log": jnp.log(jax.random.uniform(ks[17], (N_B_LAYERS, SSM_HEADS), f32, 1.0, 16.0)),
        "ssm_d": 1.0 + 0.1 * jax.random.normal(ks[18], (N_B_LAYERS, SSM_HEADS), f32),
        "ssm_norm": gain(ks[19], (N_B_LAYERS, SSM_D_INNER)),
        "ssm_w_out": nrm(ks[21], (N_B_LAYERS, SSM_D_INNER, D_MODEL), SSM_D_INNER),
    }


def reference(x, positions, ffn1_norm, ffn1_w_gate, ffn1_w_up, ffn1_w_down, mix_norm,
              ffn2_norm, ffn2_w_gate, ffn2_w_up, ffn2_w_down,
              attn_w_qkv, attn_q_norm, attn_k_norm, attn_w_o,
              ssm_w_in, ssm_conv_w, ssm_conv_b, ssm_dt_bias, ssm_a_log, ssm_d, ssm_norm, ssm_w_out):
    for i in range(DEPTH):
        x = x + 0.5 * swiglu(rms_norm(x, ffn1_norm[i]), ffn1_w_gate[i], ffn1_w_up[i], ffn1_w_down[i])
        h = rms_norm(x, mix_norm[i])
        j = i // N_MIXERS
        if i % N_MIXERS == 0:
            x = x + moba_mixer(h, positions, attn_w_qkv[j], attn_q_norm[j], attn_k_norm[j], attn_w_o[j])
        else:
            x = x + mamba2_mixer(h, ssm_w_in[j], ssm_conv_w[j], ssm_conv_b[j], ssm_dt_bias[j],
                                 ssm_a_log[j], ssm_d[j], ssm_norm[j], ssm_w_out[j])
        x = x + 0.5 * swiglu(rms_norm(x, ffn2_norm[i]), ffn2_w_gate[i], ffn2_w_up[i], ffn2_w_down[i])
    return x
```

```python
import numpy as np
from contextlib import ExitStack

import concourse.bass as bass
import concourse.mybir as mybir
from concourse.bass_utils import run_bass_kernel_spmd

F32 = mybir.dt.float32
BF16 = mybir.dt.bfloat16
I32 = mybir.dt.int32
AF = mybir.ActivationFunctionType
ALU = mybir.AluOpType
AX = mybir.AxisListType

D = 1024
DC = D // 128
DFF = 2816
FC = DFF // 128
EPS = 1e-6
NCORES = 8
NDSEM = 8


class Tile:
    __slots__ = ("name", "w", "r", "psum")

    def __init__(self, name="", psum=False):
        self.name = name
        self.w = None
        self.r = []
        self.psum = psum


class Prog:
    ENG = ("pe", "act", "dve", "pool", "sp")
    DMAQ = ("pool", "sp")

    def __init__(self, nc):
        self.nc = nc
        self.streams = {e: [] for e in self.ENG}
        self.ndma = {q: 0 for q in self.DMAQ}

    def _collect(self, eng, reads, writes, is_dma):
        deps = set()
        for t in reads:
            if t.w is not None:
                deps.add(t.w)
            if t.psum:
                for r in t.r:
                    if r[1] != eng:
                        deps.add(r)
        for t in writes:
            if t.w is not None:
                deps.add(t.w)
            for r in t.r:
                deps.add(r)
        if not is_dma:
            raw = set()
            for t in reads:
                if t.w is not None and t.w[0] == "op" and t.w[1] == eng:
                    raw.add(t.w)
            deps = {d for d in deps
                    if not (d[0] == "op" and d[1] == eng) or (d in raw and eng != "pe")}
        return deps

    def op(self, eng, name, *args, reads=(), writes=(), deps=(), **kw):
        fn = None if name is None else (name, args, kw)
        d = self._collect(eng, reads, writes, False)
        d.update(deps)
        st = self.streams[eng]
        ref = ("op", eng, len(st))
        st.append({"fn": fn, "deps": d, "dma": None, "sig": False})
        for t in reads:
            t.r.append(ref)
        for t in writes:
            t.w = ref
            t.r = []
        return ref

    def dma(self, q, reads=(), writes=(), deps=(), **kw):
        fn = ("dma_start", (), kw)
        d = self._collect(q, reads, writes, True)
        d.update(deps)
        if q == "sp":
            d.update(getattr(self, "bar_refs", ()))
        i = self.ndma[q]
        self.ndma[q] = i + 1
        if i >= NDSEM:
            d.add(("dma", q, i - NDSEM))
        st = self.streams[q]
        ref = ("dma", q, i)
        st.append({"fn": fn, "deps": d, "dma": i, "sig": False})
        for t in reads:
            t.r.append(ref)
        for t in writes:
            t.w = ref
            t.r = []
        return ref

    def last_refs(self, engines):
        out = []
        for e in engines:
            st = self.streams[e]
            for k in range(len(st) - 1, -1, -1):
                if st[k]["dma"] is None:
                    out.append(("op", e, k))
                    break
        return out

    def barrier(self, engines=("pe", "act", "dve")):
        refs = self.last_refs(engines)
        for e in engines:
            self.op(e, None, deps=[r for r in refs if r[1] != e])
        self.bar_refs = list(refs)

    def finalize_and_emit(self, block, final_wait_eng="sp"):
        nc = self.nc
        fin = set()
        for q in self.DMAQ:
            n = self.ndma[q]
            for i in range(max(0, n - NDSEM), n):
                fin.add(("dma", q, i))
        self.op(final_wait_eng, None, deps=fin)

        plans = {}
        for e in self.ENG:
            waited_op = {}
            waited_dma = {}
            plan = []
            for k, ins in enumerate(self.streams[e]):
                need_op = {}
                need_dma = {}
                for d in ins["deps"]:
                    if d[0] == "op":
                        if d[2] > need_op.get(d[1], -1):
                            need_op[d[1]] = d[2]
                    else:
                        key = (d[1], d[2] % NDSEM)
                        if d[2] > need_dma.get(key, -1):
                            need_dma[key] = d[2]
                waits = []
                for pe_, idx in need_op.items():
                    if idx > waited_op.get(pe_, -1):
                        waited_op[pe_] = idx
                        self.streams[pe_][idx]["sig"] = True
                        waits.append(("op", pe_, idx))
                for key, idx in need_dma.items():
                    if idx > waited_dma.get(key, -1):
                        waited_dma[key] = idx
                        waits.append(("dma", key[0], idx))
                plan.append(waits)
            plans[e] = plan
        ticks = {}
        for e in self.ENG:
            c = 0
            tk = []
            for ins in self.streams[e]:
                if ins["sig"]:
                    c += 1
                tk.append(c)
            ticks[e] = tk
        self.stats = {e: (len(self.streams[e]), ticks[e][-1] if ticks[e] else 0,
                          sum(len(w) for w in plans[e])) for e in self.ENG}

        sems = {e: nc.alloc_semaphore("s_" + e) for e in self.ENG}
        dsems = {q: [nc.alloc_semaphore("d_%s%d" % (q, i)) for i in range(NDSEM)]
                 for q in self.DMAQ}

        def run(e, engobj):
            plan = plans[e]
            for k, ins in enumerate(self.streams[e]):
                for w in plan[k]:
                    if w[0] == "op":
                        engobj.wait_ge(sems[w[1]], ticks[w[1]][w[2]])
                    else:
                        engobj.wait_ge(dsems[w[1]][w[2] % NDSEM], 16 * (w[2] // NDSEM + 1))
                fn = ins["fn"]
                if fn is None:
                    if ins["sig"]:
                        engobj.nop(nofuse=True).then_inc(sems[e], 1)
                    continue
                bi = getattr(engobj, fn[0])(*fn[1], **fn[2])
                if ins["dma"] is not None:
                    bi.then_inc(dsems[e][ins["dma"] % NDSEM], 16)
                elif ins["sig"]:
                    bi.then_inc(sems[e], 1)

        block.tensor(lambda eng: run("pe", eng))
        block.scalar(lambda eng: run("act", eng))
        block.vector(lambda eng: run("dve", eng))
        block.gpsimd(lambda eng: run("pool", eng))
        block.sync(lambda eng: run("sp", eng))


class Cfg:
    def __init__(self, nseq=2, T=2048, phases=None, debug=False):
        self.nseq = nseq
        self.T = T
        self.NT = T // 512
        self.phases = phases
        self.debug = debug


FULL_PHASES = ["ffn1_0", "attn", "ffn2_0", "ffn1_1", "ssm", "ffn2_1"]

COL = {}
_c = 0
for _l in range(2):
    for _n in ("ffn1_norm", "mix_norm", "ffn2_norm"):
        COL[(_n, _l)] = _c
        _c += 8
COL["q_norm"] = _c; _c += 1
COL["k_norm"] = _c; _c += 1
COL["inv_freq"] = _c; _c += 1
COL["sgn"] = _c; _c += 1
COL["ident"] = _c; _c += 128
COL["pastbias"] = _c; _c += 128
COL["convw"] = _c; _c += 128
COL["convb"] = _c; _c += 32
COL["ssm_norm"] = _c; _c += 16
COL["dskip"] = _c; _c += 16
COL["dtbias"] = _c; _c += 32
COL["alog"] = _c; _c += 32
COL["U"] = _c; _c += 128
COL["onesf"] = _c; _c += 128
COL["one"] = _c; _c += 1
NCOL = _c

CB = {}
_c = 0
CB["ident"] = _c; _c += 128
CB["swap"] = _c; _c += 128
CB["causal"] = _c; _c += 512
CB["onehot"] = _c; _c += 1024
CB["ones"] = _c; _c += 128
NCB = _c

NEG = -1.0e9
NEGM = -30000.0
HD = 128
NH = 8
TWO_PI = 6.283185307179586
CW1 = 6.28125
CW2 = TWO_PI - CW1


class Builder:
    def __init__(self, cfg):
        self.cfg = cfg
        nc = bass.Bass("TRN2", target_bir_lowering=False)
        self.nc = nc
        T, S = cfg.T, cfg.nseq
        dt = nc.dram_tensor
        self.d_xT = dt("xT", [S, D, T], F32, kind="ExternalInput").ap()
        self.d_cvec = dt("cvec", [128, NCOL], F32, kind="ExternalInput").ap()
        self.d_wg = [dt("wg%d" % i, [2, D, DFF], F32, kind="ExternalInput").ap() for i in (1, 2)]
        self.d_wu = [dt("wu%d" % i, [2, D, DFF], F32, kind="ExternalInput").ap() for i in (1, 2)]
        self.d_wd = [dt("wd%d" % i, [2, DFF, D], F32, kind="ExternalInput").ap() for i in (1, 2)]
        self.d_cbf = dt("cbf", [128, NCB], F32, kind="ExternalInput").ap()
        self.d_pos = dt("pos", [S, 128, T], I32, kind="ExternalInput").ap()
        self.d_wqkv = dt("wqkv", [NH, 3 * D, HD], F32, kind="ExternalInput").ap()
        self.d_wo = dt("wo", [D, D], F32, kind="ExternalInput").ap()
        self.d_win = dt("w_in", [D, 6176], F32, kind="ExternalInput").ap()
        self.d_wout = dt("w_out", [2048, D], F32, kind="ExternalInput").ap()
        self.d_out = dt("outT", [S, D, T], F32, kind="ExternalOutput").ap()

    def sb(self, es, name, shape, dtype):
        self._uid = getattr(self, "_uid", 0) + 1
        return es.enter_context(self.nc.sbuf_tensor("%s_%d" % (name, self._uid), shape, dtype))

    def build(self):
        nc, cfg = self.nc, self.cfg
        T, NT = cfg.T, cfg.NT
        P = Prog(nc)
        self.P = P
        with ExitStack() as es:
            self.xT = self.sb(es, "xT_sb", [128, DC, T], F32)
            self.hT_full = self.sb(es, "hT_sb", [128, DC, max(T, 2048)], BF16)
            self.hT = self.hT_full[:, :, 0:T]
            self.NSLOT = 5
            self.wring = [self.sb(es, "wring%d" % i, [128, 4096], BF16) for i in range(self.NSLOT)]
            self.wring_t = [Tile("wring%d" % i) for i in range(self.NSLOT)]
            self.wslot = 0
            self.cvec = self.sb(es, "cvec_sb", [128, NCOL], F32)
            self.ones_d = self.sb(es, "ones_d", [128, 128], BF16)
            self.t_x = [[Tile("x%d_%d" % (c, u)) for u in range(2 * NT)] for c in range(DC)]
            self.t_h = [[Tile("h%d_%d" % (c, t)) for t in range(NT)] for c in range(DC)]
            self.t_const = Tile("const")
            self.ps = [es.enter_context(nc.psum_tensor("ps%d" % i, [128, 512], F32)) for i in range(8)]
            self.t_ps = [Tile("ps%d" % i, psum=True) for i in range(8)]

            P.dma("sp", out=self.cvec[:], in_=self.d_cvec[:, :], writes=[self.t_const])
            P.op("dve", "memset", self.ones_d[:], 1.0 / D, writes=[self.t_const])
            self.ones_hd = self.sb(es, "ones_hd", [128, 128], BF16)
            P.op("dve", "memset", self.ones_hd[:], 1.0 / HD, writes=[self.t_const])
            self.cbf = self.sb(es, "cbf_sb", [128, NCB], BF16)
            P.dma("pool", out=self.cbf[:], in_=self.d_cbf[:, :], writes=[self.t_const])
            self.epsc = self.sb(es, "epsc", [128, 1], F32)
            P.op("dve", "memset", self.epsc[:], EPS, writes=[self.t_const])

            phases = cfg.phases or FULL_PHASES
            for s in range(cfg.nseq):
                for c in range(DC):
                    P.dma("sp", out=self.xT[:, c, :], in_=self.d_xT[s, c * 128:(c + 1) * 128, :],
                          writes=self.t_x[c])
                for ph in phases:
                    if ph.startswith("ffn"):
                        which = 0 if ph.startswith("ffn1") else 1
                        layer = int(ph[-1])
                        self.ffn(es, layer, which)
                    elif ph == "attn":
                        self.attn(es, s)
                    elif ph == "ssm":
                        self.ssm(es, s)
                    else:
                        raise ValueError(ph)
                    P.barrier()
                for c in range(DC):
                    P.dma("sp", out=self.d_out[s, c * 128:(c + 1) * 128, :], in_=self.xT[:, c, :],
                          reads=self.t_x[c])
            with nc.Block() as block:
                P.finalize_and_emit(block)
        return nc

    def wload(self, src_ap, shape):
        i = self.wslot
        self.wslot = (i + 1) % self.NSLOT
        a, b = shape
        assert a * b <= 4096
        dst = self.wring[i][:, 0:a * b].rearrange("p (a b) -> p a b", a=a)
        t = self.wring_t[i]
        self.P.dma("pool", out=dst, in_=src_ap, writes=[t])
        return dst, t

    def rstd_from_ms(self, out_ap, ms_ap, ms_tiles, out_tile):
        P = self.P
        P.op("act", "activation", out=out_ap, in_=ms_ap, func=AF.Ln, bias=self.epsc[:, 0:1],
             reads=list(ms_tiles) + [self.t_const], writes=[out_tile])
        P.op("act", "activation", out=out_ap, in_=out_ap, func=AF.Exp, scale=-0.5,
             reads=[out_tile], writes=[out_tile])

    def rmsnorm_to_hT(self, es2, gcol):
        P, cfg = self.P, self.cfg
        sq = [self.sb(es2, "sq%d" % i, [128, 512], BF16) for i in range(2)]
        t_sq = [Tile("sq%d" % i) for i in range(2)]
        rstd = [self.sb(es2, "rstd%d" % i, [128, 512], F32) for i in range(2)]
        t_rstd = [Tile("rstd%d" % i) for i in range(2)]
        for t in range(cfg.NT):
            ts = slice(t * 512, (t + 1) * 512)
            pb = 6 + (t % 2)
            for c in range(DC):
                k = c % 2
                P.op("act", "activation", out=sq[k][:], in_=self.xT[:, c, ts], func=AF.Square,
                     reads=[*self.t_x[c][2 * t:2 * t + 2]], writes=[t_sq[k]])
                P.op("pe", "matmul", self.ps[pb][:], self.ones_d[:], sq[k][:],
                     start=(c == 0), stop=(c == DC - 1),
                     reads=[t_sq[k], self.t_const], writes=[self.t_ps[pb]])
            r = t % 2
            self.rstd_from_ms(rstd[r][:], self.ps[pb][:], [self.t_ps[pb]], t_rstd[r])
            for c in range(DC):
                P.op("dve", "scalar_tensor_tensor", out=self.hT[:, c, ts], in0=self.xT[:, c, ts],
                     scalar=self.cvec[:, gcol + c:gcol + c + 1], in1=rstd[r][:],
                     op0=ALU.mult, op1=ALU.mult,
                     reads=[*self.t_x[c][2 * t:2 * t + 2], t_rstd[r], self.t_const], writes=[self.t_h[c][t]])

    def ffn(self, es, layer, which):
        P, cfg = self.P, self.cfg
        gname = "ffn1_norm" if which == 0 else "ffn2_norm"
        gcol = COL[(gname, layer)]
        wg, wu, wd = self.d_wg[which], self.d_wu[which], self.d_wd[which]
        with ExitStack() as es2:
            self.rmsnorm_to_hT(es2, gcol)
            sg = [self.sb(es2, "sg%d" % i, [128, 512], F32) for i in range(2)]
            t_sg = [Tile("sg%d" % i) for i in range(2)]
            aT = [self.sb(es2, "aT%d" % i, [128, 4, 512], BF16) for i in range(2)]
            t_a = [[Tile("a%d_%d" % (i, f)) for f in range(4)] for i in range(2)]
            ngroups = (FC + 3) // 4
            it = 0
            for j in range(ngroups):
                nfc = min(4, FC - 4 * j)
                ncol = nfc * 128
                c0 = j * 512
                wg_sb, t_wg = self.wload(
                    wg[layer, :, c0:c0 + ncol].rearrange("(kc p) n -> p kc n", p=128), (DC, ncol))
                wu_sb, t_wu = self.wload(
                    wu[layer, :, c0:c0 + ncol].rearrange("(kc p) n -> p kc n", p=128), (DC, ncol))
                wd_sb, t_wd = self.wload(
                    wd[layer, c0:c0 + ncol, :].rearrange("(fc p) n -> p fc n", p=128), (nfc, D))
                for t in range(cfg.NT):
                    ts = slice(t * 512, (t + 1) * 512)
                    ab = it % 2
                    it += 1
                    for f in range(nfc):
                        pg = (2 * f) % 4
                        pu = (2 * f + 1) % 4
                        fs = slice(f * 128, (f + 1) * 128)
                        for c in range(DC):
                            P.op("pe", "matmul", self.ps[pg][:], wg_sb[:, c, fs], self.hT[:, c, ts],
                                 start=(c == 0), stop=(c == DC - 1),
                                 reads=[t_wg, self.t_h[c][t]], writes=[self.t_ps[pg]])
                        for c in range(DC):
                            P.op("pe", "matmul", self.ps[pu][:], wu_sb[:, c, fs], self.hT[:, c, ts],
                                 start=(c == 0), stop=(c == DC - 1),
                                 reads=[t_wu, self.t_h[c][t]], writes=[self.t_ps[pu]])
                        k = f % 2
                        P.op("act", "activation", out=sg[k][:], in_=self.ps[pg][:], func=AF.Silu,
                             reads=[self.t_ps[pg]], writes=[t_sg[k]])
                        P.op("dve", "tensor_tensor", out=aT[ab][:, f, :], in0=self.ps[pu][:],
                             in1=sg[k][:], op=ALU.mult,
                             reads=[self.t_ps[pu], t_sg[k]], writes=[t_a[ab][f]])
                    for dc in range(DC):
                        py = 4 + (dc % 2)
                        ds = slice(dc * 128, (dc + 1) * 128)
                        for f in range(nfc):
                            P.op("pe", "matmul", self.ps[py][:], wd_sb[:, f, ds], aT[ab][:, f, :],
                                 start=(f == 0), stop=(f == nfc - 1),
                                 reads=[t_wd, t_a[ab][f]], writes=[self.t_ps[py]])
                        P.op("dve", "scalar_tensor_tensor", out=self.xT[:, dc, ts],
                             in0=self.ps[py][:], scalar=0.5, in1=self.xT[:, dc, ts],
                             op0=ALU.mult, op1=ALU.add,
                             reads=[self.t_ps[py], *self.t_x[dc][2 * t:2 * t + 2]], writes=[*self.t_x[dc][2 * t:2 * t + 2]])


    def attn(self, es, s):
        P, cfg = self.P, self.cfg
        T, NT = cfg.T, cfg.NT
        NB = T // 256
        NQC = T // 128
        scale = float(HD) ** -0.5
        cv, cb = self.cvec, self.cbf
        tc_ = self.t_const
        ps, tps = self.ps, self.t_ps
        ident_f = cv[:, COL["ident"]:COL["ident"] + 128]
        ident_b = cb[:, CB["ident"]:CB["ident"] + 128]
        swap_b = cb[:, CB["swap"]:CB["swap"] + 128]
        ones_b = cb[:, CB["ones"]:CB["ones"] + 128]
        with ExitStack() as es2:
            with ExitStack() as es3:
                self.rmsnorm_to_hT(es3, COL[("mix_norm", 0)])
            P.barrier()
            cosT = self.sb(es2, "cosT", [128, T], F32)
            sinT = self.sb(es2, "sinT", [128, T], F32)
            t_cs = [Tile("cs%d" % t) for t in range(NT)]
            with ExitStack() as es3:
                posi = self.sb(es3, "posi", [128, 512], I32)
                posf = self.sb(es3, "posf", [128, 512], F32)
                ang = self.sb(es3, "ang", [128, 512], F32)
                kf = self.sb(es3, "kf", [128, 512], F32)
                ki = self.sb(es3, "ki", [128, 512], I32)
                rr = self.sb(es3, "rr", [128, 512], F32)
                mm = self.sb(es3, "mm", [128, 512], F32)
                tl = {n: Tile(n) for n in ("posi", "posf", "ang", "kf", "ki", "rr", "mm")}
                for t in range(NT):
                    ts = slice(t * 512, (t + 1) * 512)
                    P.dma("sp", out=posi[:], in_=self.d_pos[s, :, ts], writes=[tl["posi"]])
                    P.op("dve", "tensor_copy", out=posf[:], in_=posi[:],
                         reads=[tl["posi"]], writes=[tl["posf"]])
                    for which, dst in ((0, sinT), (1, cosT)):
                        if which == 0:
                            P.op("dve", "tensor_scalar", out=ang[:], in0=posf[:],
                                 scalar1=cv[:, COL["inv_freq"]:COL["inv_freq"] + 1], scalar2=None,
                                 op0=ALU.mult, reads=[tl["posf"], tc_], writes=[tl["ang"]])
                        else:
                            P.op("dve", "tensor_scalar", out=ang[:], in0=posf[:],
                                 scalar1=cv[:, COL["inv_freq"]:COL["inv_freq"] + 1],
                                 scalar2=float(np.pi / 2), op0=ALU.mult, op1=ALU.add,
                                 reads=[tl["posf"], tc_], writes=[tl["ang"]])
                        P.op("dve", "tensor_scalar", out=ki[:], in0=ang[:], scalar1=float(1.0 / TWO_PI),
                             scalar2=None, op0=ALU.mult, reads=[tl["ang"]], writes=[tl["ki"]])
                        P.op("dve", "tensor_copy", out=kf[:], in_=ki[:], reads=[tl["ki"]], writes=[tl["kf"]])
                        P.op("dve", "scalar_tensor_tensor", out=rr[:], in0=kf[:], scalar=-CW1, in1=ang[:],
                             op0=ALU.mult, op1=ALU.add, reads=[tl["kf"], tl["ang"]], writes=[tl["rr"]])
                        P.op("dve", "scalar_tensor_tensor", out=rr[:], in0=kf[:], scalar=-CW2, in1=rr[:],
                             op0=ALU.mult, op1=ALU.add, reads=[tl["kf"], tl["rr"]], writes=[tl["rr"]])
                        P.op("dve", "tensor_scalar", out=mm[:], in0=rr[:], scalar1=float(np.pi),
                             scalar2=-TWO_PI, op0=ALU.is_gt, op1=ALU.mult,
                             reads=[tl["rr"]], writes=[tl["mm"]])
                        P.op("dve", "tensor_tensor", out=rr[:], in0=rr[:], in1=mm[:], op=ALU.add,
                             reads=[tl["rr"], tl["mm"]], writes=[tl["rr"]])
                        P.op("dve", "tensor_scalar", out=mm[:], in0=rr[:], scalar1=float(-np.pi),
                             scalar2=TWO_PI, op0=ALU.is_lt, op1=ALU.mult,
                             reads=[tl["rr"]], writes=[tl["mm"]])
                        P.op("dve", "tensor_tensor", out=rr[:], in0=rr[:], in1=mm[:], op=ALU.add,
                             reads=[tl["rr"], tl["mm"]], writes=[tl["rr"]])
                        if which == 0:
                            P.op("act", "activation", out=dst[:, ts], in_=rr[:], func=AF.Sin,
                                 scale=cv[:, COL["sgn"]:COL["sgn"] + 1],
                                 reads=[tl["rr"], tc_], writes=[t_cs[t]])
                        else:
                            P.op("act", "activation", out=dst[:, ts], in_=rr[:], func=AF.Sin,
                                 reads=[tl["rr"]], writes=[t_cs[t]])
            P.barrier()
            q_ro = self.sb(es2, "q_ro", [128, T], F32)
            q_bf = self.sb(es2, "q_bf", [128, T], BF16)
            k_bf = self.sb(es2, "k_bf", [128, T], BF16)
            v_sb = self.sb(es2, "v_sb", [128, NQC, 128], BF16)
            oT = self.sb(es2, "oT", [128, T], BF16)
            sqb = [self.sb(es2, "sqb0", [128, 512], BF16)] * 2
            rstq = [self.sb(es2, "rstq%d" % i, [128, 512], F32) for i in range(2)]
            qn = [self.sb(es2, "qn%d" % i, [128, 512], F32) for i in range(2)]
            qnb = [self.sb(es2, "qnb%d" % i, [128, 512], BF16) for i in range(2)]
            tmp = [self.sb(es2, "tmpq0", [128, 512], F32)] * 2
            kro = self.sb(es2, "kro", [128, 512], F32)
            E = [self.sb(es2, "E%d" % i, [128, 512], BF16) for i in range(2)]
            rden = self.sb(es2, "rden", [128, 256], F32)
            kmean = self.sb(es2, "kmean", [128, 8], F32)
            Gm = self.sb(es2, "Gm", [128, NQC * 8], F32)
            G2 = self.sb(es2, "G2", [128, NQC * 8], F32)
            gt = self.sb(es2, "gt", [128, NQC * 8], F32)
            mx = self.sb(es2, "mx", [128, NQC], F32)
            negm = self.sb(es2, "negm", [128, NQC * 8], F32)
            nmT = self.sb(es2, "nmT", [8, T], BF16)
            t_qro = [Tile("qro%d" % t) for t in range(NT)]
            t_qbf = [Tile("qbf%d" % t) for t in range(NT)]
            t_kbf = [Tile("kbf%d" % t) for t in range(NT)]
            t_v = [Tile("v%d" % t) for t in range(NT)]
            t_oT = [Tile("oT%d" % t) for t in range(NT)]
            t_km = Tile("kmean")
            t_nm = [Tile("nmT%d" % t) for t in range(NT)]
            tl = {n: Tile(n) for n in ("sqb0", "rstq0", "qn0", "qnb0", "tmp0", "sqb1", "rstq1", "qn1",
                                       "qnb1", "tmp1", "kro", "E0", "E1",
                                       "rden", "Gm", "G2", "gt", "mx", "negm")}
            t_E = [tl["E0"], tl["E1"]]
            g3 = lambda ap: ap.rearrange("p (a b) -> p a b", b=8)
            mxb = mx[:, :].unsqueeze(2).to_broadcast([128, NQC, 8])

            for h in range(NH):
                w_sb, t_w = self.wload(
                    self.d_wqkv[h].rearrange("(j p) n -> p j n", p=128), (24, 128))
                wo_sb, t_wo = self.wload(
                    self.d_wo[h * 128:(h + 1) * 128, :].rearrange("p (a n) -> p a n", a=1), (1, D))
                def qk_ops(qk, t):
                    b = qk
                    gcol = COL["q_norm"] if qk == 0 else COL["k_norm"]
                    ts = slice(t * 512, (t + 1) * 512)
                    pr = qk
                    pss = 2 + qk
                    dst = q_ro[:, ts] if qk == 0 else kro[:]
                    dt_ = t_qro[t] if qk == 0 else tl["kro"]
                    n = lambda x: "%s%d" % (x, 0 if x in ("sqb", "tmp") else b)
                    ops = []
                    def proj():
                        for c in range(DC):
                            P.op("pe", "matmul", ps[pr][:], w_sb[:, qk * 8 + c, :], self.hT[:, c, ts],
                                 start=(c == 0), stop=(c == DC - 1),
                                 reads=[t_w, self.t_h[c][t]], writes=[tps[pr]])
                    ops.append(proj)
                    def sq_ss():
                        P.op("act", "activation", out=sqb[b][:], in_=ps[pr][:], func=AF.Square,
                             reads=[tps[pr]], writes=[tl[n("sqb")]])
                        P.op("pe", "matmul", ps[pss][:], self.ones_hd[:], sqb[b][:],
                             start=True, stop=True,
                             reads=[tl[n("sqb")], tc_], writes=[tps[pss]])
                    ops.append(sq_ss)
                    ops.append(lambda: P.op("act", "activation", out=rstq[b][:], in_=ps[pss][:], func=AF.Ln,
                                            bias=self.epsc[:, 0:1], reads=[tps[pss], tc_],
                                            writes=[tl[n("rstq")]]))
                    ops.append(lambda: P.op("act", "activation", out=rstq[b][:], in_=rstq[b][:], func=AF.Exp,
                                            scale=-0.5, reads=[tl[n("rstq")]], writes=[tl[n("rstq")]]))
                    ops.append(lambda: P.op("dve", "scalar_tensor_tensor", out=qn[b][:], in0=ps[pr][:],
                                            scalar=cv[:, gcol:gcol + 1], in1=rstq[b][:],
                                            op0=ALU.mult, op1=ALU.mult,
                                            reads=[tps[pr], tl[n("rstq")], tc_], writes=[tl[n("qn")]]))
                    ops.append(lambda: P.op("act", "activation", out=qnb[b][:], in_=qn[b][:], func=AF.Copy,
                                            reads=[tl[n("qn")]], writes=[tl[n("qnb")]]))
                    ops.append(lambda: P.op("pe", "matmul", ps[pss][:], swap_b, qnb[b][:], start=True, stop=True,
                                            reads=[tl[n("qnb")], tc_], writes=[tps[pss]]))
                    ops.append(lambda: P.op("dve", "tensor_tensor", out=dst, in0=qn[b][:], in1=cosT[:, ts],
                                            op=ALU.mult, reads=[tl[n("qn")], t_cs[t]], writes=[dt_]))
                    def rot_add():
                        P.op("dve", "tensor_tensor", out=tmp[b][:], in0=ps[pss][:],
                             in1=sinT[:, ts], op=ALU.mult,
                             reads=[tps[pss], t_cs[t]], writes=[tl[n("tmp")]])
                        P.op("dve", "tensor_tensor", out=dst, in0=dst, in1=tmp[b][:],
                             op=ALU.add, reads=[dt_, tl[n("tmp")]], writes=[dt_])
                    ops.append(rot_add)
                    if qk == 0:
                        ops.append(lambda: P.op("act", "activation", out=q_bf[:, ts], in_=dst, func=AF.Copy,
                                                reads=[dt_], writes=[t_qbf[t]]))
                    else:
                        ops.append(lambda: P.op("act", "activation", out=k_bf[:, ts], in_=dst, func=AF.Copy,
                                                reads=[dt_], writes=[t_kbf[t]]))
                        ops.append(lambda: P.op("dve", "tensor_reduce", out=kmean[:, 2 * t:2 * t + 2],
                                                in_=kro[:, :].rearrange("p (a b) -> p a b", a=2),
                                                axis=AX.X, op=ALU.add, reads=[dt_], writes=[t_km]))
                    return ops

                for t in range(NT):
                    oa, ob = qk_ops(0, t), qk_ops(1, t)
                    for i_ in range(max(len(oa), len(ob))):
                        if i_ < len(oa):
                            oa[i_]()
                        if i_ < len(ob):
                            ob[i_]()
                for t in range(NT):
                    pv = t % 2
                    for j in range(4):
                        tc0 = t * 512 + j * 128
                        for c in range(DC):
                            P.op("pe", "matmul", ps[pv][:, j * 128:(j + 1) * 128],
                                 self.hT[:, c, tc0:tc0 + 128], w_sb[:, 16 + c, :],
                                 start=(c == 0), stop=(c == DC - 1),
                                 reads=[t_w, self.t_h[c][t]], writes=[tps[pv]])
                    P.op("act", "activation",
                         out=v_sb[:, 4 * t:4 * t + 4, :],
                         in_=ps[pv][:, :].rearrange("p (a b) -> p a b", a=4), func=AF.Copy,
                         reads=[tps[pv]], writes=[t_v[t]])
                for qc in range(NQC):
                    P.op("pe", "matmul", ps[2][:, qc * 8:(qc + 1) * 8],
                         q_ro[:, qc * 128:(qc + 1) * 128], kmean[:, :], start=True, stop=True,
                         reads=[t_qro[qc // 4], t_km], writes=[tps[2]])
                pbias = cv[:, COL["pastbias"]:COL["pastbias"] + NQC * 8]
                P.op("dve", "tensor_tensor", out=Gm[:], in0=ps[2][:, 0:NQC * 8], in1=pbias, op=ALU.add,
                     reads=[tps[2], tc_], writes=[tl["Gm"]])
                src, srct = Gm, tl["Gm"]
                for rnd in range(3):
                    P.op("dve", "tensor_reduce", out=mx[:], in_=g3(src[:, :]), axis=AX.X, op=ALU.max,
                         reads=[srct], writes=[tl["mx"]])
                    if rnd == 2:
                        break
                    P.op("dve", "tensor_tensor", out=g3(gt[:, :]), in0=g3(src[:, :]), in1=mxb, op=ALU.is_ge,
                         reads=[srct, tl["mx"]], writes=[tl["gt"]])
                    P.op("dve", "scalar_tensor_tensor", out=G2[:], in0=gt[:], scalar=NEG, in1=src[:],
                         op0=ALU.mult, op1=ALU.add, reads=[tl["gt"], srct], writes=[tl["G2"]])
                    src, srct = G2, tl["G2"]
                P.op("dve", "tensor_scalar", out=mx[:], in0=mx[:], scalar1=NEG / 2, scalar2=None,
                     op0=ALU.max, reads=[tl["mx"]], writes=[tl["mx"]])
                P.op("dve", "tensor_tensor", out=g3(gt[:, :]), in0=g3(Gm[:, :]), in1=mxb, op=ALU.is_lt,
                     reads=[tl["Gm"], tl["mx"]], writes=[tl["gt"]])
                P.op("dve", "tensor_scalar", out=negm[:], in0=gt[:], scalar1=NEGM, scalar2=None,
                     op0=ALU.mult, reads=[tl["gt"]], writes=[tl["negm"]])
                for t in range(NT):
                    for j in range(4):
                        qc = 4 * t + j
                        P.op("pe", "matmul", ps[3][0:8, j * 128:(j + 1) * 128],
                             negm[:, qc * 8:(qc + 1) * 8], ident_f, start=True, stop=True,
                             reads=[tl["negm"], tc_], writes=[tps[3]])
                    P.op("act", "activation", out=nmT[0:8, t * 512:(t + 1) * 512], in_=ps[3][0:8, :],
                         func=AF.Copy, reads=[tps[3]], writes=[t_nm[t]])
                pairs = [(qb, n) for qb in range(NB) for n in range(qb + 1)]

                def emit_qk(i):
                    qb, n = pairs[i]
                    qs = slice(qb * 256, (qb + 1) * 256)
                    tq = qb // 2
                    pS = 4 + (i % 2)
                    for kc in range(2):
                        k0 = n * 256 + kc * 128
                        cs = slice(kc * 256, (kc + 1) * 256)
                        P.op("pe", "matmul", ps[pS][:, cs], k_bf[:, k0:k0 + 128], q_bf[:, qs],
                             start=True, stop=False,
                             reads=[t_kbf[n // 2], t_qbf[tq]], writes=[tps[pS]])
                        if n == qb:
                            P.op("pe", "matmul", ps[pS][:, cs], ident_b,
                                 cb[:, CB["causal"] + kc * 256:CB["causal"] + (kc + 1) * 256],
                                 start=False, stop=True, reads=[tc_], writes=[tps[pS]])
                        else:
                            P.op("pe", "matmul", ps[pS][:, cs],
                                 cb[0:8, CB["onehot"] + n * 128:CB["onehot"] + (n + 1) * 128],
                                 nmT[0:8, qs], start=False, stop=True,
                                 reads=[tc_, t_nm[tq]], writes=[tps[pS]])
                    P.op("act", "activation", out=E[i % 2][:], in_=ps[pS][:], func=AF.Exp, scale=scale,
                         reads=[tps[pS]], writes=[t_E[i % 2]])

                def emit_pv(i):
                    qb, n = pairs[i]
                    qs = slice(qb * 256, (qb + 1) * 256)
                    tq = qb // 2
                    eb = i % 2
                    bo, bd = (6, 7) if qb % 2 == 0 else (0, 1)
                    for kc in range(2):
                        cs = slice(kc * 256, (kc + 1) * 256)
                        first = (n == 0 and kc == 0)
                        last = (n == qb and kc == 1)
                        P.op("pe", "matmul", ps[bo][:, 0:256], v_sb[:, n * 2 + kc, :], E[eb][:, cs],
                             start=first, stop=last,
                             reads=[t_v[n // 2], t_E[eb]], writes=[tps[bo]])
                        P.op("pe", "matmul", ps[bd][:, 0:256], ones_b, E[eb][:, cs],
                             start=first, stop=last,
                             reads=[tc_, t_E[eb]], writes=[tps[bd]])
                    if n == qb:
                        P.op("dve", "reciprocal", out=rden[:], in_=ps[bd][:, 0:256],
                             reads=[tps[bd]], writes=[tl["rden"]])
                        P.op("dve", "tensor_tensor", out=oT[:, qs], in0=ps[bo][:, 0:256], in1=rden[:],
                             op=ALU.mult, reads=[tps[bo], tl["rden"]], writes=[t_oT[tq]])

                for i in range(len(pairs) + 1):
                    if i < len(pairs):
                        emit_qk(i)
                    if i >= 1:
                        emit_pv(i - 1)
                for t in range(NT):
                    ts = slice(t * 512, (t + 1) * 512)
                    for dc in range(DC):
                        py = 2 + (dc % 2)
                        P.op("pe", "matmul", ps[py][:], wo_sb[:, 0, dc * 128:(dc + 1) * 128], oT[:, ts],
                             start=True, stop=True, reads=[t_wo, t_oT[t]], writes=[tps[py]])
                        P.op("dve", "tensor_tensor", out=self.xT[:, dc, ts], in0=ps[py][:],
                             in1=self.xT[:, dc, ts], op=ALU.add,
                             reads=[tps[py], *self.t_x[dc][2 * t:2 * t + 2]], writes=[*self.t_x[dc][2 * t:2 * t + 2]])


    def ssm(self, es, s):
        P, cfg = self.P, self.cfg
        T = cfg.T
        TT = 256
        NTT = T // TT
        cv, cb = self.cvec, self.cbf
        tc_ = self.t_const
        ps = self.ps
        ident_b = cb[:, CB["ident"]:CB["ident"] + 128]
        U_f = cv[:, COL["U"]:COL["U"] + 128]
        ones_f = cv[:, COL["onesf"]:COL["onesf"] + 128]
        onec = cv[:, COL["one"]:COL["one"] + 1]
        hflat = self.hT_full[:, :, :].rearrange("p a t -> p (a t)")
        with ExitStack() as es2:
            sb = lambda n, shp, d: self.sb(es2, n, shp, d)
            h_t = hflat[:, 0:2048].rearrange("p (c t) -> p c t", c=8)
            zsT = hflat[:, 2048:6144].rearrange("p (c t) -> p c t", c=16)
            xsT = hflat[:, 6144:10240].rearrange("p (c t) -> p c t", c=16)
            BT = hflat[:, 10240:12288].rearrange("p (c t) -> p c t", c=8)
            CT = hflat[:, 12288:14336].rearrange("p (c t) -> p c t", c=8)
            ynT = sb("ynT", [128, 16, TT], BF16)
            x_tok = sb("x_tok", [128, 2048], BF16)
            B_tok = sb("B_tok", [128, 1024], BF16)
            xdtd = sb("xdtd", [128, 2048], BF16)
            S = sb("S_state", [128, 2048], F32)
            prevT = sb("prevT", [128, 2048], BF16)
            hist = sb("hist", [128, 32, 3], F32)
            pre = [sb("pre%d" % i, [128, TT + 3], F32) for i in range(4)]
            acc = [sb("acc%d" % i, [128, TT], F32) for i in range(6)]
            sqs = [sb("sqs%d" % i, [128, TT], BF16) for i in range(2)]
            rst = sb("rst", [128, TT], F32)
            ones_g = sb("ones_g", [128, 128], BF16)
            ealog = sb("ealog", [128, 32], F32)
            dtx = [sb("dtx%d" % j, [128, 32], F32) for j in range(2)]
            dts = [sb("dts%d" % j, [128, 32], F32) for j in range(2)]
            adt = [sb("adt%d" % j, [128, 32], F32) for j in range(2)]
            acs = [sb("acs%d" % j, [128, 32], F32) for j in range(2)]
            wgt = [sb("wgt%d" % j, [128, 32], F32) for j in range(2)]
            cdc = [sb("cdc%d" % j, [128, 32], F32) for j in range(2)]
            t32 = [sb("t32%d" % j, [128, 32], F32) for j in range(2)]
            CBm = [sb("CBm%d" % i, [128, 128], F32) for i in range(3)]
            seg = sb("seg", [128, 512], F32)
            Lh = sb("Lh", [128, 512], F32)
            Mh = [sb("Mh%d" % i, [128, 512], BF16) for i in range(2)]
            eAB = sb("eAB", [128, 512], F32)
            Cs = [sb("Cs%d" % i, [128, 512], BF16) for i in range(2)]
            yg = [sb("yg%d" % i, [128, 2, 128], F32) for i in range(2)]
            ysq = [sb("ysq%d" % i, [128, 128], BF16) for i in range(2)]
            rsg = [sb("rsg%d" % i, [128, 128], F32) for i in range(2)]
            stmp = sb("stmp", [128, 256], F32)
            adtb = sb("adtb", [128, 4, 128], F32)
            tn = {}
            def tl(n):
                if n not in tn:
                    tn[n] = Tile(n, psum=n.startswith("ps_"))
                return tn[n]
            pt = lambda n: tl("ps_" + n)

            P.op("dve", "memset", ones_g[:], 1.0 / 256.0, writes=[tl("ones_g")])
            P.op("dve", "memset", S[:], 0.0, writes=[tl("S%d" % g) for g in range(8)])
            P.op("dve", "memset", prevT[:], 0.0, writes=[tl("prevT%d" % g) for g in range(8)])
            P.op("dve", "memset", hist[:], 0.0, writes=[tl("hist%d" % c) for c in range(32)])
            P.op("act", "activation", out=ealog[:], in_=cv[:, COL["alog"]:COL["alog"] + 32], func=AF.Exp,
                 reads=[tc_], writes=[tl("ealog")])
            gcol = COL[("mix_norm", 1)]

            pbanks = [0, 1, 3, 5]

            def conv_s1(ch, slot, w_sb, t_w, f):
                pb = (0, 1, 3)[slot % 3]
                bank = ps[pb][:, 0:TT]
                for c in range(DC):
                    P.op("pe", "matmul", bank, w_sb[:, c, f * 128:(f + 1) * 128], h_t[:, c, :],
                         start=(c == 0), stop=(c == DC - 1), reads=[t_w, tl("h_t")],
                         writes=[pt("b%d" % pb)])
                pr, ac = pre[slot % 4], acc[slot % 6]
                tpr, tac = tl("pre%d" % (slot % 4)), tl("acc%d" % (slot % 6))
                P.op("dve", "tensor_copy", out=pr[:, 0:3], in_=hist[:, ch, :],
                     reads=[tl("hist%d" % ch)], writes=[tpr])
                P.op("act", "activation", out=pr[:, 3:TT + 3], in_=bank, func=AF.Copy,
                     reads=[pt("b%d" % pb)], writes=[tpr])

            def conv_s1b(ch, slot):
                pr, ac = pre[slot % 4], acc[slot % 6]
                tpr, tac = tl("pre%d" % (slot % 4)), tl("acc%d" % (slot % 6))
                P.op("act", "activation", out=ac[:], in_=pr[:, 0:TT], func=AF.Identity,
                     scale=cv[:, COL["convw"] + ch:COL["convw"] + ch + 1],
                     bias=cv[:, COL["convb"] + ch:COL["convb"] + ch + 1],
                     reads=[tpr, tc_], writes=[tac])

            def conv_tap(ch, slot, k):
                pr, ac = pre[slot % 4], acc[slot % 6]
                tpr, tac = tl("pre%d" % (slot % 4)), tl("acc%d" % (slot % 6))
                P.op("dve", "scalar_tensor_tensor", out=ac[:], in0=pr[:, k:k + TT],
                     scalar=cv[:, COL["convw"] + k * 32 + ch:COL["convw"] + k * 32 + ch + 1],
                     in1=ac[:], op0=ALU.mult, op1=ALU.add,
                     reads=[tpr, tac, tc_], writes=[tac])

            def conv_hist(ch, slot):
                pr = pre[slot % 4]
                P.op("dve", "tensor_copy", out=hist[:, ch, :], in_=pr[:, TT:TT + 3],
                     reads=[tl("pre%d" % (slot % 4))], writes=[tl("hist%d" % ch)])

            def conv_s3(ch, slot):
                ac, tac = acc[slot % 6], tl("acc%d" % (slot % 6))
                if ch < 16:
                    dst, dtile = xsT[:, ch, :], tl("xsT")
                elif ch < 24:
                    dst, dtile = BT[:, ch - 16, :], tl("BT")
                else:
                    dst, dtile = CT[:, ch - 24, :], tl("CT")
                P.op("act", "activation", out=dst, in_=ac[:], func=AF.Silu,
                     reads=[tac], writes=[dtile])

            def z_chunk(fc, slot, wz):
                grp, f = fc // 4, fc % 4
                if f == 0:
                    c0 = grp * 512
                    wz[grp] = self.wload(
                        self.d_win[:, c0:c0 + 512].rearrange("(kc p) n -> p kc n", p=128), (DC, 512))
                w_sb, t_w = wz[grp]
                pb = 5
                bank = ps[pb][:, 0:TT]
                for c in range(DC):
                    P.op("pe", "matmul", bank, w_sb[:, c, f * 128:(f + 1) * 128], h_t[:, c, :],
                         start=(c == 0), stop=(c == DC - 1), reads=[t_w, tl("h_t")],
                         writes=[pt("b%d" % pb)])
                P.op("act", "activation", out=zsT[:, fc, :], in_=bank, func=AF.Silu,
                     reads=[pt("b%d" % pb)], writes=[tl("zsT")])

            def conv_phase():
                wsb = {}
                wz = {}
                npair = 16
                for p_ in range(npair + 2):
                    if p_ < npair:
                        z_chunk(p_, 2 * p_ + 1, wz)
                    if p_ < npair:
                        chs = (2 * p_, 2 * p_ + 1)
                        for ch in chs:
                            grp = 4 + ch // 4
                            if ch % 4 == 0:
                                c0 = grp * 512
                                wsb[grp] = self.wload(
                                    self.d_win[:, c0:c0 + 512].rearrange("(kc p) n -> p kc n", p=128),
                                    (DC, 512))
                            conv_s1(ch, ch, wsb[grp][0], wsb[grp][1], ch % 4)
                        for ch in chs:
                            conv_s1b(ch, ch)
                    if 1 <= p_ <= npair:
                        chs = (2 * (p_ - 1), 2 * (p_ - 1) + 1)
                        for k in (1, 2, 3):
                            for ch in chs:
                                conv_tap(ch, ch, k)
                        for ch in chs:
                            conv_hist(ch, ch)
                    if p_ >= 2:
                        for ch in (2 * (p_ - 2), 2 * (p_ - 2) + 1):
                            conv_s3(ch, ch)

            def z_phase(itc):
                for grp in range(4):
                    c0 = grp * 512
                    w_sb, t_w = self.wload(
                        self.d_win[:, c0:c0 + 512].rearrange("(kc p) n -> p kc n", p=128), (DC, 512))
                    for f in range(4):
                        pb = pbanks[itc[0] % 4]
                        itc[0] += 1
                        bank = ps[pb][:, 0:TT]
                        for c in range(DC):
                            P.op("pe", "matmul", bank, w_sb[:, c, f * 128:(f + 1) * 128], h_t[:, c, :],
                                 start=(c == 0), stop=(c == DC - 1), reads=[t_w, tl("h_t")],
                                 writes=[pt("b%d" % pb)])
                        fc = grp * 4 + f
                        P.op("act", "activation", out=zsT[:, fc, :], in_=bank, func=AF.Silu,
                             reads=[pt("b%d" % pb)], writes=[tl("zsT")])

            for tt in range(NTT):
                t0 = tt * TT
                tsl = slice(t0, t0 + TT)
                xtile = [self.t_x[c][t0 // 256] for c in range(DC)]
                for c in range(DC):
                    k = c % 2
                    P.op("act", "activation", out=sqs[k][:], in_=self.xT[:, c, tsl], func=AF.Square,
                         reads=[xtile[c]], writes=[tl("sqs%d" % k)])
                    P.op("pe", "matmul", ps[2][:, 0:TT], self.ones_d[:], sqs[k][:],
                         start=(c == 0), stop=(c == DC - 1), reads=[tl("sqs%d" % k), tc_], writes=[pt("b2")])
                self.rstd_from_ms(rst[:], ps[2][:, 0:TT], [pt("b2")], tl("rst"))
                for c in range(DC):
                    P.op("dve", "scalar_tensor_tensor", out=h_t[:, c, :], in0=self.xT[:, c, tsl],
                         scalar=cv[:, gcol + c:gcol + c + 1], in1=rst[:], op0=ALU.mult, op1=ALU.mult,
                         reads=[xtile[c], tl("rst"), tc_], writes=[tl("h_t")])
                wdt, t_wdt = self.wload(
                    self.d_win[:, 6144:6176].rearrange("(kc p) n -> p kc n", p=128), (DC, 32))
                for j in range(2):
                    js = slice(j * 128, (j + 1) * 128)
                    dcol = slice(256 + 32 * j, 288 + 32 * j)
                    for c in range(DC):
                        P.op("pe", "matmul", ps[2][:, dcol], h_t[:, c, js], wdt[:, c, :],
                             start=(c == 0), stop=(c == DC - 1), reads=[tl("h_t"), t_wdt],
                             writes=[pt("b2")])
                for j in range(2):
                    dcol = slice(256 + 32 * j, 288 + 32 * j)
                    P.op("dve", "tensor_tensor", out=dtx[j][:], in0=ps[2][:, dcol],
                         in1=cv[:, COL["dtbias"]:COL["dtbias"] + 32], op=ALU.add,
                         reads=[pt("b2"), tc_], writes=[tl("dtx%d" % j)])
                for j in range(2):
                    P.op("act", "activation", out=dtx[j][:], in_=dtx[j][:], func=AF.Exp,
                         reads=[tl("dtx%d" % j)], writes=[tl("dtx%d" % j)])
                for j in range(2):
                    P.op("act", "activation", out=dts[j][:], in_=dtx[j][:], func=AF.Ln, bias=onec,
                         reads=[tl("dtx%d" % j), tc_], writes=[tl("dts%d" % j)])
                for j in range(2):
                    P.op("dve", "scalar_tensor_tensor", out=adt[j][:], in0=dts[j][:], scalar=-1.0,
                         in1=ealog[:], op0=ALU.mult, op1=ALU.mult,
                         reads=[tl("dts%d" % j), tl("ealog")], writes=[tl("adt%d" % j)])
                itc = [0]
                conv_phase()
                for j in range(2):
                    P.op("pe", "matmul", ps[2][:, 320 + 64 * j:352 + 64 * j], U_f, adt[j][:],
                         start=True, stop=True, reads=[tl("adt%d" % j), tc_], writes=[pt("b2")])
                    P.op("pe", "matmul", ps[2][:, 352 + 64 * j:384 + 64 * j], ones_f, adt[j][:],
                         start=True, stop=True, reads=[tl("adt%d" % j), tc_], writes=[pt("b2")])
                for j in range(2):
                    a_ps = ps[2][:, 320 + 64 * j:352 + 64 * j]
                    t_ps_ = ps[2][:, 352 + 64 * j:384 + 64 * j]
                    P.op("dve", "tensor_copy", out=acs[j][:], in_=a_ps,
                         reads=[pt("b2")], writes=[tl("acs%d" % j)])
                    P.op("dve", "tensor_tensor", out=t32[j][:], in0=t_ps_, in1=acs[j][:],
                         op=ALU.subtract, reads=[pt("b2"), tl("acs%d" % j)], writes=[tl("t32%d" % j)])
                    P.op("act", "activation", out=cdc[j][:], in_=t_ps_, func=AF.Exp,
                         reads=[pt("b2")], writes=[tl("cdc%d" % j)])
                for j in range(2):
                    P.op("act", "activation", out=t32[j][:], in_=t32[j][:], func=AF.Exp,
                         reads=[tl("t32%d" % j)], writes=[tl("t32%d" % j)])
                for j in range(2):
                    P.op("dve", "tensor_tensor", out=wgt[j][:], in0=t32[j][:], in1=dts[j][:], op=ALU.mult,
                         reads=[tl("t32%d" % j), tl("dts%d" % j)], writes=[tl("wgt%d" % j)])
                wo_pieces = {}
                for chf in range(2):
                    for rh in range(2):
                        wo_pieces[(chf, rh)] = self.wload(
                            self.d_wout[rh * 1024:(rh + 1) * 1024, chf * 512:(chf + 1) * 512].rearrange(
                                "(a p) n -> p a n", p=128), (8, 512))
                for j in range(2):
                    js = slice(j * 128, (j + 1) * 128)
                    for q4 in range(4):
                        for f in range(4):
                            fc = q4 * 4 + f
                            P.op("pe", "matmul", ps[3][:, f * 128:(f + 1) * 128], xsT[:, fc, js], ident_b,
                                 start=True, stop=True, reads=[tl("xsT"), tc_], writes=[pt("b3")])
                        P.op("act", "activation", out=x_tok[:, q4 * 512:(q4 + 1) * 512], in_=ps[3][:],
                             func=AF.Copy, reads=[pt("b3")], writes=[tl("x_tok")])
                    for q4 in range(2):
                        for f in range(4):
                            g = q4 * 4 + f
                            P.op("pe", "matmul", ps[3][:, f * 128:(f + 1) * 128], BT[:, g, js], ident_b,
                                 start=True, stop=True, reads=[tl("BT"), tc_], writes=[pt("b3")])
                        P.op("act", "activation", out=B_tok[:, q4 * 512:(q4 + 1) * 512], in_=ps[3][:],
                             func=AF.Copy, reads=[pt("b3")], writes=[tl("B_tok")])
                    P.op("dve", "tensor_tensor",
                         out=xdtd[:, :].rearrange("p (h d) -> p h d", h=32),
                         in0=x_tok[:, :].rearrange("p (h d) -> p h d", h=32),
                         in1=wgt[j][:, :].unsqueeze(2).to_broadcast([128, 32, 64]), op=ALU.mult,
                         reads=[tl("x_tok"), tl("wgt%d" % j)], writes=[tl("xdtd")])

                    def f1(g):
                        gb3 = g % 3
                        abk = 7 if g % 2 == 0 else 5
                        P.op("pe", "matmul", ps[4][:, 0:128], BT[:, g, js], CT[:, g, js],
                             start=True, stop=True, reads=[tl("BT"), tl("CT")], writes=[pt("b4")])
                        P.op("dve", "tensor_tensor", out=CBm[gb3][:], in0=ps[4][:, 0:128], in1=U_f, op=ALU.mult,
                             reads=[pt("b4"), tc_], writes=[tl("CBm%d" % gb3)])
                        for r in range(4):
                            hh = 4 * g + r
                            P.op("pe", "matmul", ps[abk][:, r * 128:(r + 1) * 128],
                                 adt[j][:, hh:hh + 1].to_broadcast([128, 128]), U_f,
                                 start=True, stop=True, reads=[tl("adt%d" % j), tc_], writes=[pt("b%d" % abk)])

                    def f2a(g):
                        gb = g % 2
                        abk = 7 if g % 2 == 0 else 5
                        for r in range(4):
                            hh = 4 * g + r
                            P.op("dve", "tensor_scalar", out=seg[:, r * 128:(r + 1) * 128],
                                 in0=ps[abk][:, r * 128:(r + 1) * 128], scalar1=acs[j][:, hh:hh + 1],
                                 scalar2=0.0, op0=ALU.subtract, op1=ALU.min,
                                 reads=[pt("b%d" % abk), tl("acs%d" % j)], writes=[tl("seg")])
                        P.op("act", "activation", out=Lh[:], in_=seg[:], func=AF.Exp,
                             reads=[tl("seg")], writes=[tl("Lh")])
                        P.op("act", "activation", out=eAB[:], in_=ps[abk][:], func=AF.Exp,
                             reads=[pt("b%d" % abk)], writes=[tl("eAB")])
                        P.op("dve", "tensor_tensor",
                             out=Cs[gb][:, :].rearrange("p (r l) -> p r l", r=4),
                             in0=eAB[:, :].rearrange("p (r l) -> p r l", r=4),
                             in1=CT[:, g, js].unsqueeze(1).to_broadcast([128, 4, 128]), op=ALU.mult,
                             reads=[tl("eAB"), tl("CT")], writes=[tl("Cs%d" % gb)])

                    def f2b(g):
                        gb = g % 2
                        gb3 = g % 3
                        for r in range(4):
                            hh = 4 * g + r
                            P.op("dve", "scalar_tensor_tensor", out=Mh[gb][:, r * 128:(r + 1) * 128],
                                 in0=Lh[:, r * 128:(r + 1) * 128], scalar=dts[j][:, hh:hh + 1], in1=CBm[gb3][:],
                                 op0=ALU.mult, op1=ALU.mult,
                                 reads=[tl("Lh"), tl("dts%d" % j), tl("CBm%d" % gb3)], writes=[tl("Mh%d" % gb)])

                    def back_a(g):
                        gb = g % 2
                        for r in range(4):
                            hh = 4 * g + r
                            half = (hh % 2) * 64
                            ybank = (hh // 2) % 2
                            yreg = ps[ybank][half:half + 64, 0:128]
                            ytile = pt("b%d" % ybank)
                            P.op("pe", "matmul", yreg, x_tok[:, hh * 64:(hh + 1) * 64],
                                 Mh[gb][:, r * 128:(r + 1) * 128], start=True, stop=False,
                                 tile_position=(0, half),
                                 reads=[tl("x_tok"), tl("Mh%d" % gb)], writes=[ytile])
                            P.op("pe", "matmul", yreg, prevT[:, hh * 64:(hh + 1) * 64],
                                 Cs[gb][:, r * 128:(r + 1) * 128], start=False, stop=True,
                                 tile_position=(0, half),
                                 reads=[tl("prevT%d" % g), tl("Cs%d" % gb)], writes=[ytile])
                            if hh % 2 == 1:
                                fc = hh // 2
                                fi = fc % 2
                                ygt = tl("yg%d_%d" % (gb, fi))
                                yps = ps[ybank][:, 0:128]
                                P.op("dve", "scalar_tensor_tensor", out=yg[gb][:, fi, :], in0=xsT[:, fc, js],
                                     scalar=cv[:, COL["dskip"] + fc:COL["dskip"] + fc + 1], in1=yps,
                                     op0=ALU.mult, op1=ALU.add,
                                     reads=[tl("xsT"), ytile, tc_], writes=[ygt])
                        gs_ = slice(g * 256, (g + 1) * 256)
                        P.op("pe", "matmul", ps[6][:, 0:256], B_tok[:, g * 128:(g + 1) * 128], xdtd[:, gs_],
                             start=True, stop=True, reads=[tl("B_tok"), tl("xdtd")], writes=[pt("b6")])
                        for fi in (0, 1):
                            fc = 2 * g + fi
                            ygt = tl("yg%d_%d" % (gb, fi))
                            P.op("dve", "tensor_tensor", out=yg[gb][:, fi, :], in0=yg[gb][:, fi, :],
                                 in1=zsT[:, fc, js], op=ALU.mult,
                                 reads=[ygt, tl("zsT")], writes=[ygt])
                        for fi in (0, 1):
                            ygt = tl("yg%d_%d" % (gb, fi))
                            P.op("act", "activation", out=ysq[fi][:], in_=yg[gb][:, fi, :], func=AF.Square,
                                 reads=[ygt], writes=[tl("ysq%d" % fi)])
                            P.op("pe", "matmul", ps[2][:, 0:128], ones_g[:], ysq[fi][:],
                                 start=(fi == 0), stop=(fi == 1),
                                 reads=[tl("ysq%d" % fi), tl("ones_g")], writes=[pt("b2")])
                        self.rstd_from_ms(rsg[gb][:], ps[2][:, 0:128], [pt("b2")], tl("rsg%d" % gb))

                    def back_b(g):
                        gb = g % 2
                        gs_ = slice(g * 256, (g + 1) * 256)
                        P.op("dve", "tensor_tensor",
                             out=stmp[:, :].rearrange("p (r d) -> p r d", r=4),
                             in0=S[:, gs_].rearrange("p (r d) -> p r d", r=4),
                             in1=cdc[j][:, 4 * g:4 * g + 4].unsqueeze(2).to_broadcast([128, 4, 64]),
                             op=ALU.mult, reads=[tl("S%d" % g), tl("cdc%d" % j)], writes=[tl("stmp")])
                        P.op("dve", "tensor_tensor", out=S[:, gs_], in0=stmp[:], in1=ps[6][:, 0:256],
                             op=ALU.add, reads=[tl("stmp"), pt("b6")], writes=[tl("S%d" % g)])
                        P.op("act", "activation", out=prevT[:, gs_], in_=S[:, gs_], func=AF.Copy,
                             reads=[tl("S%d" % g)], writes=[tl("prevT%d" % g)])
                        for fi in (0, 1):
                            fc = 2 * g + fi
                            P.op("dve", "scalar_tensor_tensor", out=ynT[:, fc, js], in0=yg[gb][:, fi, :],
                                 scalar=cv[:, COL["ssm_norm"] + fc:COL["ssm_norm"] + fc + 1], in1=rsg[gb][:],
                                 op0=ALU.mult, op1=ALU.mult,
                                 reads=[tl("yg%d_%d" % (gb, fi)), tl("rsg%d" % gb), tc_], writes=[tl("ynT")])

                    for step in range(-2, 8):
                        if 0 <= step + 2 < 8:
                            f1(step + 2)
                        if 0 <= step + 1 < 8:
                            f2a(step + 1)
                        if step >= 0:
                            back_a(step)
                        if 0 <= step + 1 < 8:
                            f2b(step + 1)
                        if step >= 0:
                            back_b(step)
                obanks = [3, 4, 6, 7]
                for chf in range(2):
                    for rh in range(2):
                        wo_sb, t_wo = wo_pieces[(chf, rh)]
                        for d4 in range(4):
                            bk = obanks[d4]
                            for f in range(8):
                                fc = rh * 8 + f
                                P.op("pe", "matmul", ps[bk][:, 0:TT], wo_sb[:, f, d4 * 128:(d4 + 1) * 128],
                                     ynT[:, fc, :], start=(rh == 0 and f == 0), stop=(rh == 1 and f == 7),
                                     reads=[t_wo, tl("ynT")], writes=[pt("b%d" % bk)])
                    for d4 in range(4):
                        dc = chf * 4 + d4
                        bk = obanks[d4]
                        P.op("dve", "tensor_tensor", out=self.xT[:, dc, tsl], in0=ps[bk][:, 0:TT],
                             in1=self.xT[:, dc, tsl], op=ALU.add,
                             reads=[pt("b%d" % bk), xtile[dc]], writes=[xtile[dc]])


def make_cvec(inp):
    cv = np.zeros((128, NCOL), np.float32)
    for l in range(2):
        for n in ("ffn1_norm", "mix_norm", "ffn2_norm"):
            c0 = COL[(n, l)]
            cv[:, c0:c0 + 8] = np.asarray(inp[n][l], np.float32).reshape(8, 128).T
    cv[:, COL["q_norm"]] = np.asarray(inp["attn_q_norm"][0], np.float32)
    cv[:, COL["k_norm"]] = np.asarray(inp["attn_k_norm"][0], np.float32)
    half = HD // 2
    invf = (10000.0 ** (-np.arange(half, dtype=np.float32) / half)).astype(np.float32)
    cv[:, COL["inv_freq"]] = np.concatenate([invf, invf])
    cv[:, COL["sgn"]] = np.concatenate([-np.ones(half), np.ones(half)]).astype(np.float32)
    cv[:, COL["ident"]:COL["ident"] + 128] = np.eye(128, dtype=np.float32)
    cw = np.asarray(inp["ssm_conv_w"][0], np.float32)
    for k in range(4):
        cv[:, COL["convw"] + k * 32:COL["convw"] + (k + 1) * 32] = cw[k].reshape(32, 128).T
    cv[:, COL["convb"]:COL["convb"] + 32] = np.asarray(inp["ssm_conv_b"][0], np.float32).reshape(32, 128).T
    cv[:, COL["ssm_norm"]:COL["ssm_norm"] + 16] = np.asarray(inp["ssm_norm"][0], np.float32).reshape(16, 128).T
    cv[:, COL["dskip"]:COL["dskip"] + 16] = np.repeat(
        np.asarray(inp["ssm_d"][0], np.float32), 64).reshape(16, 128).T
    cv[:, COL["dtbias"]:COL["dtbias"] + 32] = np.asarray(inp["ssm_dt_bias"][0], np.float32)[None, :]
    cv[:, COL["alog"]:COL["alog"] + 32] = np.asarray(inp["ssm_a_log"][0], np.float32)[None, :]
    cv[:, COL["U"]:COL["U"] + 128] = np.triu(np.ones((128, 128), np.float32))
    cv[:, COL["onesf"]:COL["onesf"] + 128] = 1.0
    cv[:, COL["one"]] = 1.0
    pb = np.zeros((16, 8), np.float32)
    for qc in range(16):
        pb[qc, (qc // 2):] = NEG
    cv[:, COL["pastbias"]:COL["pastbias"] + 128] = pb.reshape(1, 128)
    return cv


def make_cbf():
    cbm = np.zeros((128, NCB), np.float32)
    cbm[:, CB["ident"]:CB["ident"] + 128] = np.eye(128)
    sw = np.zeros((128, 128), np.float32)
    for m in range(128):
        sw[(m + 64) % 128, m] = 1.0
    cbm[:, CB["swap"]:CB["swap"] + 128] = sw
    kk = np.arange(128)[:, None]
    qq = np.arange(256)[None, :]
    for kc in range(2):
        cbm[:, CB["causal"] + kc * 256:CB["causal"] + (kc + 1) * 256] = np.where(
            kc * 128 + kk <= qq, 0.0, NEGM)
    for n in range(8):
        cbm[n, CB["onehot"] + n * 128:CB["onehot"] + (n + 1) * 128] = 1.0
    cbm[:, CB["ones"]:CB["ones"] + 128] = 1.0
    return cbm


def make_in_maps(inp, cfg, ncores):
    f = lambda a: np.ascontiguousarray(np.asarray(a, dtype=np.float32))
    shared = {
        "cvec": make_cvec(inp),
        "wg1": f(inp["ffn1_w_gate"]), "wu1": f(inp["ffn1_w_up"]), "wd1": f(inp["ffn1_w_down"]),
        "wg2": f(inp["ffn2_w_gate"]), "wu2": f(inp["ffn2_w_up"]), "wd2": f(inp["ffn2_w_down"]),
    }
    wqkv = f(inp["attn_w_qkv"])[0]
    shared["wqkv"] = np.ascontiguousarray(
        wqkv.reshape(D, 3, NH, HD).transpose(2, 1, 0, 3).reshape(NH, 3 * D, HD))
    shared["wo"] = f(inp["attn_w_o"])[0]
    shared["cbf"] = make_cbf()
    shared["w_in"] = f(inp["ssm_w_in"])[0]
    shared["w_out"] = f(inp["ssm_w_out"])[0]
    pos = np.asarray(inp["positions"]).astype(np.int32)
    x = np.asarray(inp["x"], np.float32)
    maps = []
    for i in range(ncores):
        xs = x[i * cfg.nseq:(i + 1) * cfg.nseq, :cfg.T]
        m = dict(shared)
        m["xT"] = np.ascontiguousarray(xs.transpose(0, 2, 1))
        ps_ = pos[i * cfg.nseq:(i + 1) * cfg.nseq, :cfg.T]
        m["pos"] = np.ascontiguousarray(np.broadcast_to(ps_[:, None, :], (cfg.nseq, 128, cfg.T)))
        maps.append(m)
    return maps


_NC_CACHE = {}


def run(inp, cfg, ncores, trace=False):
    key = (cfg.nseq, cfg.T, tuple(cfg.phases or ()))
    if key not in _NC_CACHE:
        _NC_CACHE[key] = Builder(cfg).build()
    nc = _NC_CACHE[key]
    maps = make_in_maps(inp, cfg, ncores)
    res = run_bass_kernel_spmd(nc, maps, core_ids=list(range(ncores)), trace=trace)
    outs = [np.asarray(r["outT"]).transpose(0, 2, 1) for r in res.results]
    return np.concatenate(outs, axis=0), res


def kernel(**inputs):
    cfg = Cfg(nseq=2, T=2048)
    out, _ = run(inputs, cfg, NCORES)
    return np.ascontiguousarray(out.astype(np.float32))
```

```python
import numpy as np
from contextlib import ExitStack

import concourse.bass as bass
import concourse.mybir as mybir
from concourse.bass_utils import run_bass_kernel_spmd

F32 = mybir.dt.float32
BF16 = mybir.dt.bfloat16
I32 = mybir.dt.int32
AF = mybir.ActivationFunctionType
ALU = mybir.AluOpType
AX = mybir.AxisListType

D = 1024
DC = D // 128
DFF = 2816
FC = DFF // 128
EPS = 1e-6
NCORES = 8
NDSEM = 8


class Tile:
    __slots__ = ("name", "w", "r", "psum")

    def __init__(self, name="", psum=False):
        self.name = name
        self.w = None
        self.r = []
        self.psum = psum


class Prog:
    ENG = ("pe", "act", "dve", "pool", "sp")
    DMAQ = ("pool", "sp")

    def __init__(self, nc):
        self.nc = nc
        self.streams = {e: [] for e in self.ENG}
        self.ndma = {q: 0 for q in self.DMAQ}

    def _collect(self, eng, reads, writes, is_dma):
        deps = set()
        for t in reads:
            if t.w is not None:
                deps.add(t.w)
            if t.psum:
                for r in t.r:
                    if r[1] != eng:
                        deps.add(r)
        for t in writes:
            if t.w is not None:
                deps.add(t.w)
            for r in t.r:
                deps.add(r)
        if not is_dma:
            raw = set()
            for t in reads:
                if t.w is not None and t.w[0] == "op" and t.w[1] == eng:
                    raw.add(t.w)
            deps = {d for d in deps
                    if not (d[0] == "op" and d[1] == eng) or (d in raw and eng != "pe")}
        return deps

    def op(self, eng, name, *args, reads=(), writes=(), deps=(), **kw):
        fn = None if name is None else (name, args, kw)
        d = self._collect(eng, reads, writes, False)
        d.update(deps)
        st = self.streams[eng]
        ref = ("op", eng, len(st))
        st.append({"fn": fn, "deps": d, "dma": None, "sig": False})
        for t in reads:
            t.r.append(ref)
        for t in writes:
            t.w = ref
            t.r = []
        return ref

    def dma(self, q, reads=(), writes=(), deps=(), **kw):
        fn = ("dma_start", (), kw)
        d = self._collect(q, reads, writes, True)
        d.update(deps)
        if q == "sp":
            d.update(getattr(self, "bar_refs", ()))
        i = self.ndma[q]
        self.ndma[q] = i + 1
        if i >= NDSEM:
            d.add(("dma", q, i - NDSEM))
        st = self.streams[q]
        ref = ("dma", q, i)
        st.append({"fn": fn, "deps": d, "dma": i, "sig": False})
        for t in reads:
            t.r.append(ref)
        for t in writes:
            t.w = ref
            t.r = []
        return ref

    def last_refs(self, engines):
        out = []
        for e in engines:
            st = self.streams[e]
            for k in range(len(st) - 1, -1, -1):
                if st[k]["dma"] is None:
                    out.append(("op", e, k))
                    break
        return out

    def barrier(self, engines=("pe", "act", "dve")):
        refs = self.last_refs(engines)
        for e in engines:
            self.op(e, None, deps=[r for r in refs if r[1] != e])
        self.bar_refs = list(refs)

    def finalize_and_emit(self, block, final_wait_eng="sp"):
        nc = self.nc
        fin = set()
        for q in self.DMAQ:
            n = self.ndma[q]
            for i in range(max(0, n - NDSEM), n):
                fin.add(("dma", q, i))
        self.op(final_wait_eng, None, deps=fin)

        plans = {}
        for e in self.ENG:
            waited_op = {}
            waited_dma = {}
            plan = []
            for k, ins in enumerate(self.streams[e]):
                need_op = {}
                need_dma = {}
                for d in ins["deps"]:
                    if d[0] == "op":
                        if d[2] > need_op.get(d[1], -1):
                            need_op[d[1]] = d[2]
                    else:
                        key = (d[1], d[2] % NDSEM)
                        if d[2] > need_dma.get(key, -1):
                            need_dma[key] = d[2]
                waits = []
                for pe_, idx in need_op.items():
                    if idx > waited_op.get(pe_, -1):
                        waited_op[pe_] = idx
                        self.streams[pe_][idx]["sig"] = True
                        waits.append(("op", pe_, idx))
                for key, idx in need_dma.items():
                    if idx > waited_dma.get(key, -1):
                        waited_dma[key] = idx
                        waits.append(("dma", key[0], idx))
                plan.append(waits)
            plans[e] = plan
        ticks = {}
        for e in self.ENG:
            c = 0
            tk = []
            for ins in self.streams[e]:
                if ins["sig"]:
                    c += 1
                tk.append(c)
            ticks[e] = tk
        self.stats = {e: (len(self.streams[e]), ticks[e][-1] if ticks[e] else 0,
                          sum(len(w) for w in plans[e])) for e in self.ENG}

        sems = {e: nc.alloc_semaphore("s_" + e) for e in self.ENG}
        dsems = {q: [nc.alloc_semaphore("d_%s%d" % (q, i)) for i in range(NDSEM)]
                 for q in self.DMAQ}

        def run(e, engobj):
            plan = plans[e]
            for k, ins in enumerate(self.streams[e]):
                for w in plan[k]:
                    if w[0] == "op":
                        engobj.wait_ge(sems[w[1]], ticks[w[1]][w[2]])
                    else:
                        engobj.wait_ge(dsems[w[1]][w[2] % NDSEM], 16 * (w[2] // NDSEM + 1))
                fn = ins["fn"]
                if fn is None:
                    if ins["sig"]:
                        engobj.nop(nofuse=True).then_inc(sems[e], 1)
                    continue
                bi = getattr(engobj, fn[0])(*fn[1], **fn[2])
                if ins["dma"] is not None:
                    bi.then_inc(dsems[e][ins["dma"] % NDSEM], 16)
                elif ins["sig"]:
                    bi.then_inc(sems[e], 1)

        block.tensor(lambda eng: run("pe", eng))
        block.scalar(lambda eng: run("act", eng))
        block.vector(lambda eng: run("dve", eng))
        block.gpsimd(lambda eng: run("pool", eng))
        block.sync(lambda eng: run("sp", eng))


class Cfg:
    def __init__(self, nseq=2, T=2048, phases=None, debug=False):
        self.nseq = nseq
        self.T = T
        self.NT = T // 512
        self.phases = phases
        self.debug = debug


FULL_PHASES = ["ffn1_0", "attn", "ffn2_0", "ffn1_1", "ssm", "ffn2_1"]

COL = {}
_c = 0
for _l in range(2):
    for _n in ("ffn1_norm", "mix_norm", "ffn2_norm"):
        COL[(_n, _l)] = _c
        _c += 8
COL["q_norm"] = _c; _c += 1
COL["k_norm"] = _c; _c += 1
COL["inv_freq"] = _c; _c += 1
COL["sgn"] = _c; _c += 1
COL["ident"] = _c; _c += 128
COL["pastbias"] = _c; _c += 128
COL["convw"] = _c; _c += 128
COL["convb"] = _c; _c += 32
COL["ssm_norm"] = _c; _c += 16
COL["dskip"] = _c; _c += 16
COL["dtbias"] = _c; _c += 32
COL["alog"] = _c; _c += 32
COL["U"] = _c; _c += 128
COL["onesf"] = _c; _c += 128
COL["one"] = _c; _c += 1
NCOL = _c

CB = {}
_c = 0
CB["ident"] = _c; _c += 128
CB["swap"] = _c; _c += 128
CB["causal"] = _c; _c += 512
CB["onehot"] = _c; _c += 1024
CB["ones"] = _c; _c += 128
NCB = _c

NEG = -1.0e9
NEGM = -30000.0
HD = 128
NH = 8
TWO_PI = 6.283185307179586
CW1 = 6.28125
CW2 = TWO_PI - CW1


class Builder:
    def __init__(self, cfg):
        self.cfg = cfg
        nc = bass.Bass("TRN2", target_bir_lowering=False)
        self.nc = nc
        T, S = cfg.T, cfg.nseq
        dt = nc.dram_tensor
        self.d_xT = dt("xT", [S, D, T], F32, kind="ExternalInput").ap()
        self.d_cvec = dt("cvec", [128, NCOL], F32, kind="ExternalInput").ap()
        self.d_wg = [dt("wg%d" % i, [2, D, DFF], F32, kind="ExternalInput").ap() for i in (1, 2)]
        self.d_wu = [dt("wu%d" % i, [2, D, DFF], F32, kind="ExternalInput").ap() for i in (1, 2)]
        self.d_wd = [dt("wd%d" % i, [2, DFF, D], F32, kind="ExternalInput").ap() for i in (1, 2)]
        self.d_cbf = dt("cbf", [128, NCB], F32, kind="ExternalInput").ap()
        self.d_pos = dt("pos", [S, 128, T], I32, kind="ExternalInput").ap()
        self.d_wqkv = dt("wqkv", [NH, 3 * D, HD], F32, kind="ExternalInput").ap()
        self.d_wo = dt("wo", [D, D], F32, kind="ExternalInput").ap()
        self.d_win = dt("w_in", [D, 6176], F32, kind="ExternalInput").ap()
        self.d_wout = dt("w_out", [2048, D], F32, kind="ExternalInput").ap()
        self.d_out = dt("outT", [S, D, T], F32, kind="ExternalOutput").ap()

    def sb(self, es, name, shape, dtype):
        self._uid = getattr(self, "_uid", 0) + 1
        return es.enter_context(self.nc.sbuf_tensor("%s_%d" % (name, self._uid), shape, dtype))

    def build(self):
        nc, cfg = self.nc, self.cfg
        T, NT = cfg.T, cfg.NT
        P = Prog(nc)
        self.P = P
        with ExitStack() as es:
            self.xT = self.sb(es, "xT_sb", [128, DC, T], F32)
            self.hT_full = self.sb(es, "hT_sb", [128, DC, max(T, 2048)], BF16)
            self.hT = self.hT_full[:, :, 0:T]
            self.NSLOT = 5
            self.wring = [self.sb(es, "wring%d" % i, [128, 4096], BF16) for i in range(self.NSLOT)]
            self.wring_t = [Tile("wring%d" % i) for i in range(self.NSLOT)]
            self.wslot = 0
            self.cvec = self.sb(es, "cvec_sb", [128, NCOL], F32)
            self.ones_d = self.sb(es, "ones_d", [128, 128], BF16)
            self.t_x = [[Tile("x%d_%d" % (c, u)) for u in range(2 * NT)] for c in range(DC)]
            self.t_h = [[Tile("h%d_%d" % (c, t)) for t in range(NT)] for c in range(DC)]
            self.t_const = Tile("const")
            self.ps = [es.enter_context(nc.psum_tensor("ps%d" % i, [128, 512], F32)) for i in range(8)]
            self.t_ps = [Tile("ps%d" % i, psum=True) for i in range(8)]

            P.dma("sp", out=self.cvec[:], in_=self.d_cvec[:, :], writes=[self.t_const])
            P.op("dve", "memset", self.ones_d[:], 1.0 / D, writes=[self.t_const])
            self.ones_hd = self.sb(es, "ones_hd", [128, 128], BF16)
            P.op("dve", "memset", self.ones_hd[:], 1.0 / HD, writes=[self.t_const])
            self.cbf = self.sb(es, "cbf_sb", [128, NCB], BF16)
            P.dma("pool", out=self.cbf[:], in_=self.d_cbf[:, :], writes=[self.t_const])
            self.epsc = self.sb(es, "epsc", [128, 1], F32)
            P.op("dve", "memset", self.epsc[:], EPS, writes=[self.t_const])

            phases = cfg.phases or FULL_PHASES
            for s in range(cfg.nseq):
                for c in range(DC):
                    P.dma("sp", out=self.xT[:, c, :], in_=self.d_xT[s, c * 128:(c + 1) * 128, :],
                          writes=self.t_x[c])
                for ph in phases:
                    if ph.startswith("ffn"):
                        which = 0 if ph.startswith("ffn1") else 1
                        layer = int(ph[-1])
                        self.ffn(es, layer, which)
                    elif ph == "attn":
                        self.attn(es, s)
                    elif ph == "ssm":
                        self.ssm(es, s)
                    else:
                        raise ValueError(ph)
                    P.barrier()
                for c in range(DC):
                    P.dma("sp", out=self.d_out[s, c * 128:(c + 1) * 128, :], in_=self.xT[:, c, :],
                          reads=self.t_x[c])
            with nc.Block() as block:
                P.finalize_and_emit(block)
        return nc

    def wload(self, src_ap, shape):
        i = self.wslot
        self.wslot = (i + 1) % self.NSLOT
        a, b = shape
        assert a * b <= 4096
        dst = self.wring[i][:, 0:a * b].rearrange("p (a b) -> p a b", a=a)
        t = self.wring_t[i]
        self.P.dma("pool", out=dst, in_=src_ap, writes=[t])
        return dst, t

    def rstd_from_ms(self, out_ap, ms_ap, ms_tiles, out_tile):
        P = self.P
        P.op("act", "activation", out=out_ap, in_=ms_ap, func=AF.Ln, bias=self.epsc[:, 0:1],
             reads=list(ms_tiles) + [self.t_const], writes=[out_tile])
        P.op("act", "activation", out=out_ap, in_=out_ap, func=AF.Exp, scale=-0.5,
             reads=[out_tile], writes=[out_tile])

    def rmsnorm_to_hT(self, es2, gcol):
        P, cfg = self.P, self.cfg
        sq = [self.sb(es2, "sq%d" % i, [128, 512], BF16) for i in range(2)]
        t_sq = [Tile("sq%d" % i) for i in range(2)]
        rstd = [self.sb(es2, "rstd%d" % i, [128, 512], F32) for i in range(2)]
        t_rstd = [Tile("rstd%d" % i) for i in range(2)]
        for t in range(cfg.NT):
            ts = slice(t * 512, (t + 1) * 512)
            pb = 6 + (t % 2)
            for c in range(DC):
                k = c % 2
                P.op("act", "activation", out=sq[k][:], in_=self.xT[:, c, ts], func=AF.Square,
                     reads=[*self.t_x[c][2 * t:2 * t + 2]], writes=[t_sq[k]])
                P.op("pe", "matmul", self.ps[pb][:], self.ones_d[:], sq[k][:],
                     start=(c == 0), stop=(c == DC - 1),
                     reads=[t_sq[k], self.t_const], writes=[self.t_ps[pb]])
            r = t % 2
            self.rstd_from_ms(rstd[r][:], self.ps[pb][:], [self.t_ps[pb]], t_rstd[r])
            for c in range(DC):
                P.op("dve", "scalar_tensor_tensor", out=self.hT[:, c, ts], in0=self.xT[:, c, ts],
                     scalar=self.cvec[:, gcol + c:gcol + c + 1], in1=rstd[r][:],
                     op0=ALU.mult, op1=ALU.mult,
                     reads=[*self.t_x[c][2 * t:2 * t + 2], t_rstd[r], self.t_const], writes=[self.t_h[c][t]])

    def ffn(self, es, layer, which):
        P, cfg = self.P, self.cfg
        gname = "ffn1_norm" if which == 0 else "ffn2_norm"
        gcol = COL[(gname, layer)]
        wg, wu, wd = self.d_wg[which], self.d_wu[which], self.d_wd[which]
        with ExitStack() as es2:
            self.rmsnorm_to_hT(es2, gcol)
            sg = [self.sb(es2, "sg%d" % i, [128, 512], F32) for i in range(2)]
            t_sg = [Tile("sg%d" % i) for i in range(2)]
            aT = [self.sb(es2, "aT%d" % i, [128, 4, 512], BF16) for i in range(2)]
            t_a = [[Tile("a%d_%d" % (i, f)) for f in range(4)] for i in range(2)]
            ngroups = (FC + 3) // 4
            it = 0
            for j in range(ngroups):
                nfc = min(4, FC - 4 * j)
                ncol = nfc * 128
                c0 = j * 512
                wg_sb, t_wg = self.wload(
                    wg[layer, :, c0:c0 + ncol].rearrange("(kc p) n -> p kc n", p=128), (DC, ncol))
                wu_sb, t_wu = self.wload(
                    wu[layer, :, c0:c0 + ncol].rearrange("(kc p) n -> p kc n", p=128), (DC, ncol))
                wd_sb, t_wd = self.wload(
                    wd[layer, c0:c0 + ncol, :].rearrange("(fc p) n -> p fc n", p=128), (nfc, D))
                for t in range(cfg.NT):
                    ts = slice(t * 512, (t + 1) * 512)
                    ab = it % 2
                    it += 1
                    for f in range(nfc):
                        pg = (2 * f) % 4
                        pu = (2 * f + 1) % 4
                        fs = slice(f * 128, (f + 1) * 128)
                        for c in range(DC):
                            P.op("pe", "matmul", self.ps[pg][:], wg_sb[:, c, fs], self.hT[:, c, ts],
                                 start=(c == 0), stop=(c == DC - 1),
                                 reads=[t_wg, self.t_h[c][t]], writes=[self.t_ps[pg]])
                        for c in range(DC):
                            P.op("pe", "matmul", self.ps[pu][:], wu_sb[:, c, fs], self.hT[:, c, ts],
                                 start=(c == 0), stop=(c == DC - 1),
                                 reads=[t_wu, self.t_h[c][t]], writes=[self.t_ps[pu]])
                        k = f % 2
                        P.op("act", "activation", out=sg[k][:], in_=self.ps[pg][:], func=AF.Silu,
                             reads=[self.t_ps[pg]], writes=[t_sg[k]])
                        P.op("dve", "tensor_tensor", out=aT[ab][:, f, :], in0=self.ps[pu][:],
                             in1=sg[k][:], op=ALU.mult,
                             reads=[self.t_ps[pu], t_sg[k]], writes=[t_a[ab][f]])
                    for dc in range(DC):
                        py = 4 + (dc % 2)
                        ds = slice(dc * 128, (dc + 1) * 128)
                        for f in range(nfc):
                            P.op("pe", "matmul", self.ps[py][:], wd_sb[:, f, ds], aT[ab][:, f, :],
                                 start=(f == 0), stop=(f == nfc - 1),
                                 reads=[t_wd, t_a[ab][f]], writes=[self.t_ps[py]])
                        P.op("dve", "scalar_tensor_tensor", out=self.xT[:, dc, ts],
                             in0=self.ps[py][:], scalar=0.5, in1=self.xT[:, dc, ts],
                             op0=ALU.mult, op1=ALU.add,
                             reads=[self.t_ps[py], *self.t_x[dc][2 * t:2 * t + 2]], writes=[*self.t_x[dc][2 * t:2 * t + 2]])


    def attn(self, es, s):
        P, cfg = self.P, self.cfg
        T, NT = cfg.T, cfg.NT
        NB = T // 256
        NQC = T // 128
        scale = float(HD) ** -0.5
        cv, cb = self.cvec, self.cbf
        tc_ = self.t_const
        ps, tps = self.ps, self.t_ps
        ident_f = cv[:, COL["ident"]:COL["ident"] + 128]
        ident_b = cb[:, CB["ident"]:CB["ident"] + 128]
        swap_b = cb[:, CB["swap"]:CB["swap"] + 128]
        ones_b = cb[:, CB["ones"]:CB["ones"] + 128]
        with ExitStack() as es2:
            with ExitStack() as es3:
                self.rmsnorm_to_hT(es3, COL[("mix_norm", 0)])
            P.barrier()
            cosT = self.sb(es2, "cosT", [128, T], F32)
            sinT = self.sb(es2, "sinT", [128, T], F32)
            t_cs = [Tile("cs%d" % t) for t in range(NT)]
            with ExitStack() as es3:
                posi = self.sb(es3, "posi", [128, 512], I32)
                posf = self.sb(es3, "posf", [128, 512], F32)
                ang = self.sb(es3, "ang", [128, 512], F32)
                kf = self.sb(es3, "kf", [128, 512], F32)
                ki = self.sb(es3, "ki", [128, 512], I32)
                rr = self.sb(es3, "rr", [128, 512], F32)
                mm = self.sb(es3, "mm", [128, 512], F32)
                tl = {n: Tile(n) for n in ("posi", "posf", "ang", "kf", "ki", "rr", "mm")}
                for t in range(NT):
                    ts = slice(t * 512, (t + 1) * 512)
                    P.dma("sp", out=posi[:], in_=self.d_pos[s, :, ts], writes=[tl["posi"]])
                    P.op("dve", "tensor_copy", out=posf[:], in_=posi[:],
                         reads=[tl["posi"]], writes=[tl["posf"]])
                    for which, dst in ((0, sinT), (1, cosT)):
                        if which == 0:
                            P.op("dve", "tensor_scalar", out=ang[:], in0=posf[:],
                                 scalar1=cv[:, COL["inv_freq"]:COL["inv_freq"] + 1], scalar2=None,
                                 op0=ALU.mult, reads=[tl["posf"], tc_], writes=[tl["ang"]])
                        else:
                            P.op("dve", "tensor_scalar", out=ang[:], in0=posf[:],
                                 scalar1=cv[:, COL["inv_freq"]:COL["inv_freq"] + 1],
                                 scalar2=float(np.pi / 2), op0=ALU.mult, op1=ALU.add,
                                 reads=[tl["posf"], tc_], writes=[tl["ang"]])
                        P.op("dve", "tensor_scalar", out=ki[:], in0=ang[:], scalar1=float(1.0 / TWO_PI),
                             scalar2=None, op0=ALU.mult, reads=[tl["ang"]], writes=[tl["ki"]])
                        P.op("dve", "tensor_copy", out=kf[:], in_=ki[:], reads=[tl["ki"]], writes=[tl["kf"]])
                        P.op("dve", "scalar_tensor_tensor", out=rr[:], in0=kf[:], scalar=-CW1, in1=ang[:],
                             op0=ALU.mult, op1=ALU.add, reads=[tl["kf"], tl["ang"]], writes=[tl["rr"]])
                        P.op("dve", "scalar_tensor_tensor", out=rr[:], in0=kf[:], scalar=-CW2, in1=rr[:],
                             op0=ALU.mult, op1=ALU.add, reads=[tl["kf"], tl["rr"]], writes=[tl["rr"]])
                        P.op("dve", "tensor_scalar", out=mm[:], in0=rr[:], scalar1=float(np.pi),
                             scalar2=-TWO_PI, op0=ALU.is_gt, op1=ALU.mult,
                             reads=[tl["rr"]], writes=[tl["mm"]])
                        P.op("dve", "tensor_tensor", out=rr[:], in0=rr[:], in1=mm[:], op=ALU.add,
                             reads=[tl["rr"], tl["mm"]], writes=[tl["rr"]])
                        P.op("dve", "tensor_scalar", out=mm[:], in0=rr[:], scalar1=float(-np.pi),
                             scalar2=TWO_PI, op0=ALU.is_lt, op1=ALU.mult,
                             reads=[tl["rr"]], writes=[tl["mm"]])
                        P.op("dve", "tensor_tensor", out=rr[:], in0=rr[:], in1=mm[:], op=ALU.add,
                             reads=[tl["rr"], tl["mm"]], writes=[tl["rr"]])
                        if which == 0:
                            P.op("act", "activation", out=dst[:, ts], in_=rr[:], func=AF.Sin,
                                 scale=cv[:, COL["sgn"]:COL["sgn"] + 1],
                                 reads=[tl["rr"], tc_], writes=[t_cs[t]])
                        else:
                            P.op("act", "activation", out=dst[:, ts], in_=rr[:], func=AF.Sin,
                                 reads=[tl["rr"]], writes=[t_cs[t]])
            P.barrier()
            q_ro = self.sb(es2, "q_ro", [128, T], F32)
            q_bf = self.sb(es2, "q_bf", [128, T], BF16)
            k_bf = self.sb(es2, "k_bf", [128, T], BF16)
            v_sb = self.sb(es2, "v_sb", [128, NQC, 128], BF16)
            oT = self.sb(es2, "oT", [128, T], BF16)
            sqb = [self.sb(es2, "sqb0", [128, 512], BF16)] * 2
            rstq = [self.sb(es2, "rstq%d" % i, [128, 512], F32) for i in range(2)]
            qn = [self.sb(es2, "qn%d" % i, [128, 512], F32) for i in range(2)]
            qnb = [self.sb(es2, "qnb%d" % i, [128, 512], BF16) for i in range(2)]
            tmp = [self.sb(es2, "tmpq0", [128, 512], F32)] * 2
            kro = self.sb(es2, "kro", [128, 512], F32)
            E = [self.sb(es2, "E%d" % i, [128, 512], BF16) for i in range(2)]
            rden = self.sb(es2, "rden", [128, 256], F32)
            kmean = self.sb(es2, "kmean", [128, 8], F32)
            Gm = self.sb(es2, "Gm", [128, NQC * 8], F32)
            G2 = self.sb(es2, "G2", [128, NQC * 8], F32)
            gt = self.sb(es2, "gt", [128, NQC * 8], F32)
            mx = self.sb(es2, "mx", [128, NQC], F32)
            negm = self.sb(es2, "negm", [128, NQC * 8], F32)
            nmT = self.sb(es2, "nmT", [8, T], BF16)
            t_qro = [Tile("qro%d" % t) for t in range(NT)]
            t_qbf = [Tile("qbf%d" % t) for t in range(NT)]
            t_kbf = [Tile("kbf%d" % t) for t in range(NT)]
            t_v = [Tile("v%d" % t) for t in range(NT)]
            t_oT = [Tile("oT%d" % t) for t in range(NT)]
            t_km = Tile("kmean")
            t_nm = [Tile("nmT%d" % t) for t in range(NT)]
            tl = {n: Tile(n) for n in ("sqb0", "rstq0", "qn0", "qnb0", "tmp0", "sqb1", "rstq1", "qn1",
                                       "qnb1", "tmp1", "kro", "E0", "E1",
                                       "rden", "Gm", "G2", "gt", "mx", "negm")}
            t_E = [tl["E0"], tl["E1"]]
            g3 = lambda ap: ap.rearrange("p (a b) -> p a b", b=8)
            mxb = mx[:, :].unsqueeze(2).to_broadcast([128, NQC, 8])

            for h in range(NH):
                w_sb, t_w = self.wload(
                    self.d_wqkv[h].rearrange("(j p) n -> p j n", p=128), (24, 128))
                wo_sb, t_wo = self.wload(
                    self.d_wo[h * 128:(h + 1) * 128, :].rearrange("p (a n) -> p a n", a=1), (1, D))
                def qk_ops(qk, t):
                    b = qk
                    gcol = COL["q_norm"] if qk == 0 else COL["k_norm"]
                    ts = slice(t * 512, (t + 1) * 512)
                    pr = qk
                    pss = 2 + qk
                    dst = q_ro[:, ts] if qk == 0 else kro[:]
                    dt_ = t_qro[t] if qk == 0 else tl["kro"]
                    n = lambda x: "%s%d" % (x, 0 if x in ("sqb", "tmp") else b)
                    ops = []
                    def proj():
                        for c in range(DC):
                            P.op("pe", "matmul", ps[pr][:], w_sb[:, qk * 8 + c, :], self.hT[:, c, ts],
                                 start=(c == 0), stop=(c == DC - 1),
                                 reads=[t_w, self.t_h[c][t]], writes=[tps[pr]])
                    ops.append(proj)
                    def sq_ss():
                        P.op("act", "activation", out=sqb[b][:], in_=ps[pr][:], func=AF.Square,
                             reads=[tps[pr]], writes=[tl[n("sqb")]])
                        P.op("pe", "matmul", ps[pss][:], self.ones_hd[:], sqb[b][:],
                             start=True, stop=True,
                             reads=[tl[n("sqb")], tc_], writes=[tps[pss]])
                    ops.append(sq_ss)
                    ops.append(lambda: P.op("act", "activation", out=rstq[b][:], in_=ps[pss][:], func=AF.Ln,
                                            bias=self.epsc[:, 0:1], reads=[tps[pss], tc_],
                                            writes=[tl[n("rstq")]]))
                    ops.append(lambda: P.op("act", "activation", out=rstq[b][:], in_=rstq[b][:], func=AF.Exp,
                                            scale=-0.5, reads=[tl[n("rstq")]], writes=[tl[n("rstq")]]))
                    ops.append(lambda: P.op("dve", "scalar_tensor_tensor", out=qn[b][:], in0=ps[pr][:],
                                            scalar=cv[:, gcol:gcol + 1], in1=rstq[b][:],
                                            op0=ALU.mult, op1=ALU.mult,
                                            reads=[tps[pr], tl[n("rstq")], tc_], writes=[tl[n("qn")]]))
                    ops.append(lambda: P.op("act", "activation", out=qnb[b][:], in_=qn[b][:], func=AF.Copy,
                                            reads=[tl[n("qn")]], writes=[tl[n("qnb")]]))
                    ops.append(lambda: P.op("pe", "matmul", ps[pss][:], swap_b, qnb[b][:], start=True, stop=True,
                                            reads=[tl[n("qnb")], tc_], writes=[tps[pss]]))
                    ops.append(lambda: P.op("dve", "tensor_tensor", out=dst, in0=qn[b][:], in1=cosT[:, ts],
                                            op=ALU.mult, reads=[tl[n("qn")], t_cs[t]], writes=[dt_]))
                    def rot_add():
                        P.op("dve", "tensor_tensor", out=tmp[b][:], in0=ps[pss][:],
                             in1=sinT[:, ts], op=ALU.mult,
                             reads=[tps[pss], t_cs[t]], writes=[tl[n("tmp")]])
                        P.op("dve", "tensor_tensor", out=dst, in0=dst, in1=tmp[b][:],
                             op=ALU.add, reads=[dt_, tl[n("tmp")]], writes=[dt_])
                    ops.append(rot_add)
                    if qk == 0:
                        ops.append(lambda: P.op("act", "activation", out=q_bf[:, ts], in_=dst, func=AF.Copy,
                                                reads=[dt_], writes=[t_qbf[t]]))
                    else:
                        ops.append(lambda: P.op("act", "activation", out=k_bf[:, ts], in_=dst, func=AF.Copy,
                                                reads=[dt_], writes=[t_kbf[t]]))
                        ops.append(lambda: P.op("dve", "tensor_reduce", out=kmean[:, 2 * t:2 * t + 2],
                                                in_=kro[:, :].rearrange("p (a b) -> p a b", a=2),
                                                axis=AX.X, op=ALU.add, reads=[dt_], writes=[t_km]))
                    return ops

                for t in range(NT):
                    oa, ob = qk_ops(0, t), qk_ops(1, t)
                    for i_ in range(max(len(oa), len(ob))):
                        if i_ < len(oa):
                            oa[i_]()
                        if i_ < len(ob):
                            ob[i_]()
                for qc in range(NQC):
                    P.op("pe", "matmul", ps[2][:, qc * 8:(qc + 1) * 8],
                         q_ro[:, qc * 128:(qc + 1) * 128], kmean[:, :], start=True, stop=True,
                         reads=[t_qro[qc // 4], t_km], writes=[tps[2]])
                for t in range(NT):
                    pv = t % 2
                    for j in range(4):
                        tc0 = t * 512 + j * 128
                        for c in range(DC):
                            P.op("pe", "matmul", ps[pv][:, j * 128:(j + 1) * 128],
                                 self.hT[:, c, tc0:tc0 + 128], w_sb[:, 16 + c, :],
                                 start=(c == 0), stop=(c == DC - 1),
                                 reads=[t_w, self.t_h[c][t]], writes=[tps[pv]])
                    P.op("act", "activation",
                         out=v_sb[:, 4 * t:4 * t + 4, :],
                         in_=ps[pv][:, :].rearrange("p (a b) -> p a b", a=4), func=AF.Copy,
                         reads=[tps[pv]], writes=[t_v[t]])
                pbias = cv[:, COL["pastbias"]:COL["pastbias"] + NQC * 8]
                P.op("dve", "tensor_tensor", out=Gm[:], in0=ps[2][:, 0:NQC * 8], in1=pbias, op=ALU.add,
                     reads=[tps[2], tc_], writes=[tl["Gm"]])
                src, srct = Gm, tl["Gm"]
                for rnd in range(3):
                    P.op("dve", "tensor_reduce", out=mx[:], in_=g3(src[:, :]), axis=AX.X, op=ALU.max,
                         reads=[srct], writes=[tl["mx"]])
                    if rnd == 2:
                        break
                    P.op("dve", "tensor_tensor", out=g3(gt[:, :]), in0=g3(src[:, :]), in1=mxb, op=ALU.is_ge,
                         reads=[srct, tl["mx"]], writes=[tl["gt"]])
                    P.op("dve", "scalar_tensor_tensor", out=G2[:], in0=gt[:], scalar=NEG, in1=src[:],
                         op0=ALU.mult, op1=ALU.add, reads=[tl["gt"], srct], writes=[tl["G2"]])
                    src, srct = G2, tl["G2"]
                P.op("dve", "tensor_scalar", out=mx[:], in0=mx[:], scalar1=NEG / 2, scalar2=None,
                     op0=ALU.max, reads=[tl["mx"]], writes=[tl["mx"]])
                P.op("dve", "tensor_tensor", out=g3(gt[:, :]), in0=g3(Gm[:, :]), in1=mxb, op=ALU.is_lt,
                     reads=[tl["Gm"], tl["mx"]], writes=[tl["gt"]])
                P.op("dve", "tensor_scalar", out=negm[:], in0=gt[:], scalar1=NEGM, scalar2=None,
                     op0=ALU.mult, reads=[tl["gt"]], writes=[tl["negm"]])
                for t in range(NT):
                    for j in range(4):
                        qc = 4 * t + j
                        P.op("pe", "matmul", ps[3][0:8, j * 128:(j + 1) * 128],
                             negm[:, qc * 8:(qc + 1) * 8], ident_f, start=True, stop=True,
                             reads=[tl["negm"], tc_], writes=[tps[3]])
                    P.op("act", "activation", out=nmT[0:8, t * 512:(t + 1) * 512], in_=ps[3][0:8, :],
                         func=AF.Copy, reads=[tps[3]], writes=[t_nm[t]])
                pairs = [(qb, n) for qb in range(NB) for n in range(qb + 1)]

                def emit_qk(i):
                    qb, n = pairs[i]
                    qs = slice(qb * 256, (qb + 1) * 256)
                    tq = qb // 2
                    pS = 4 + (i % 2)
                    for kc in range(2):
                        k0 = n * 256 + kc * 128
                        cs = slice(kc * 256, (kc + 1) * 256)
                        P.op("pe", "matmul", ps[pS][:, cs], k_bf[:, k0:k0 + 128], q_bf[:, qs],
                             start=True, stop=False,
                             reads=[t_kbf[n // 2], t_qbf[tq]], writes=[tps[pS]])
                        if n == qb:
                            P.op("pe", "matmul", ps[pS][:, cs], ident_b,
                                 cb[:, CB["causal"] + kc * 256:CB["causal"] + (kc + 1) * 256],
                                 start=False, stop=True, reads=[tc_], writes=[tps[pS]])
                        else:
                            P.op("pe", "matmul", ps[pS][:, cs],
                                 cb[0:8, CB["onehot"] + n * 128:CB["onehot"] + (n + 1) * 128],
                                 nmT[0:8, qs], start=False, stop=True,
                                 reads=[tc_, t_nm[tq]], writes=[tps[pS]])
                    P.op("act", "activation", out=E[i % 2][:], in_=ps[pS][:], func=AF.Exp, scale=scale,
                         reads=[tps[pS]], writes=[t_E[i % 2]])

                def emit_pv(i):
                    qb, n = pairs[i]
                    qs = slice(qb * 256, (qb + 1) * 256)
                    tq = qb // 2
                    eb = i % 2
                    bo, bd = (6, 7) if qb % 2 == 0 else (0, 1)
                    for kc in range(2):
                        cs = slice(kc * 256, (kc + 1) * 256)
                        first = (n == 0 and kc == 0)
                        last = (n == qb and kc == 1)
                        P.op("pe", "matmul", ps[bo][:, 0:256], v_sb[:, n * 2 + kc, :], E[eb][:, cs],
                             start=first, stop=last,
                             reads=[t_v[n // 2], t_E[eb]], writes=[tps[bo]])
                        P.op("pe", "matmul", ps[bd][:, 0:256], ones_b, E[eb][:, cs],
                             start=first, stop=last,
                             reads=[tc_, t_E[eb]], writes=[tps[bd]])
                    if n == qb:
                        P.op("dve", "reciprocal", out=rden[:], in_=ps[bd][:, 0:256],
                             reads=[tps[bd]], writes=[tl["rden"]])
                        P.op("dve", "tensor_tensor", out=oT[:, qs], in0=ps[bo][:, 0:256], in1=rden[:],
                             op=ALU.mult, reads=[tps[bo], tl["rden"]], writes=[t_oT[tq]])

                for i in range(len(pairs) + 1):
                    if i < len(pairs):
                        emit_qk(i)
                    if i >= 1:
                        emit_pv(i - 1)
                for t in range(NT):
                    ts = slice(t * 512, (t + 1) * 512)
                    for dc in range(DC):
                        py = 2 + (dc % 2)
                        P.op("pe", "matmul", ps[py][:], wo_sb[:, 0, dc * 128:(dc + 1) * 128], oT[:, ts],
                             start=True, stop=True, reads=[t_wo, t_oT[t]], writes=[tps[py]])
                        P.op("dve", "tensor_tensor", out=self.xT[:, dc, ts], in0=ps[py][:],
                             in1=self.xT[:, dc, ts], op=ALU.add,
                             reads=[tps[py], *self.t_x[dc][2 * t:2 * t + 2]], writes=[*self.t_x[dc][2 * t:2 * t + 2]])


    def ssm(self, es, s):
        P, cfg = self.P, self.cfg
        T = cfg.T
        TT = 256
        NTT = T // TT
        cv, cb = self.cvec, self.cbf
        tc_ = self.t_const
        ps = self.ps
        ident_b = cb[:, CB["ident"]:CB["ident"] + 128]
        U_f = cv[:, COL["U"]:COL["U"] + 128]
        ones_f = cv[:, COL["onesf"]:COL["onesf"] + 128]
        onec = cv[:, COL["one"]:COL["one"] + 1]
        hflat = self.hT_full[:, :, :].rearrange("p a t -> p (a t)")
        with ExitStack() as es2:
            sb = lambda n, shp, d: self.sb(es2, n, shp, d)
            h_t = hflat[:, 0:2048].rearrange("p (c t) -> p c t", c=8)
            zsT = hflat[:, 2048:6144].rearrange("p (c t) -> p c t", c=16)
            xsT = hflat[:, 6144:10240].rearrange("p (c t) -> p c t", c=16)
            BT = hflat[:, 10240:12288].rearrange("p (c t) -> p c t", c=8)
            CT = hflat[:, 12288:14336].rearrange("p (c t) -> p c t", c=8)
            ynT = sb("ynT", [128, 16, TT], BF16)
            x_tok = sb("x_tok", [128, 2048], BF16)
            B_tok = sb("B_tok", [128, 1024], BF16)
            xdtd = sb("xdtd", [128, 2048], BF16)
            S = sb("S_state", [128, 2048], F32)
            prevT = sb("prevT", [128, 2048], BF16)
            hist = sb("hist", [128, 32, 3], F32)
            pre = [sb("pre%d" % i, [128, TT + 3], F32) for i in range(4)]
            acc = [sb("acc%d" % i, [128, TT], F32) for i in range(6)]
            sqs = [sb("sqs%d" % i, [128, TT], BF16) for i in range(2)]
            rst = sb("rst", [128, TT], F32)
            ones_g = sb("ones_g", [128, 128], BF16)
            ealog = sb("ealog", [128, 32], F32)
            dtx = [sb("dtx%d" % j, [128, 32], F32) for j in range(2)]
            dts = [sb("dts%d" % j, [128, 32], F32) for j in range(2)]
            adt = [sb("adt%d" % j, [128, 32], F32) for j in range(2)]
            acs = [sb("acs%d" % j, [128, 32], F32) for j in range(2)]
            wgt = [sb("wgt%d" % j, [128, 32], F32) for j in range(2)]
            cdc = [sb("cdc%d" % j, [128, 32], F32) for j in range(2)]
            t32 = [sb("t32%d" % j, [128, 32], F32) for j in range(2)]
            CBm = [sb("CBm%d" % i, [128, 128], F32) for i in range(3)]
            seg = sb("seg", [128, 512], F32)
            Lh = sb("Lh", [128, 512], F32)
            Mh = [sb("Mh%d" % i, [128, 512], BF16) for i in range(2)]
            eAB = sb("eAB", [128, 512], F32)
            Cs = [sb("Cs%d" % i, [128, 512], BF16) for i in range(2)]
            yg = [sb("yg%d" % i, [128, 2, 128], F32) for i in range(2)]
            ysq = [sb("ysq%d" % i, [128, 128], BF16) for i in range(2)]
            rsg = [sb("rsg%d" % i, [128, 128], F32) for i in range(2)]
            stmp = sb("stmp", [128, 256], F32)
            adtb = sb("adtb", [128, 4, 128], F32)
            tn = {}
            def tl(n):
                if n not in tn:
                    tn[n] = Tile(n, psum=n.startswith("ps_"))
                return tn[n]
            pt = lambda n: tl("ps_" + n)

            P.op("dve", "memset", ones_g[:], 1.0 / 256.0, writes=[tl("ones_g")])
            P.op("dve", "memset", S[:], 0.0, writes=[tl("S%d" % g) for g in range(8)])
            P.op("dve", "memset", prevT[:], 0.0, writes=[tl("prevT%d" % g) for g in range(8)])
            P.op("dve", "memset", hist[:], 0.0, writes=[tl("hist%d" % c) for c in range(32)])
            P.op("act", "activation", out=ealog[:], in_=cv[:, COL["alog"]:COL["alog"] + 32], func=AF.Exp,
                 reads=[tc_], writes=[tl("ealog")])
            gcol = COL[("mix_norm", 1)]

            pbanks = [0, 1, 3, 5]

            def conv_s1(ch, slot, w_sb, t_w, f):
                pb = (0, 1, 3)[slot % 3]
                bank = ps[pb][:, 0:TT]
                for c in range(DC):
                    P.op("pe", "matmul", bank, w_sb[:, c, f * 128:(f + 1) * 128], h_t[:, c, :],
                         start=(c == 0), stop=(c == DC - 1), reads=[t_w, tl("h_t")],
                         writes=[pt("b%d" % pb)])
                pr, ac = pre[slot % 4], acc[slot % 6]
                tpr, tac = tl("pre%d" % (slot % 4)), tl("acc%d" % (slot % 6))
                P.op("dve", "tensor_copy", out=pr[:, 0:3], in_=hist[:, ch, :],
                     reads=[tl("hist%d" % ch)], writes=[tpr])
                P.op("act", "activation", out=pr[:, 3:TT + 3], in_=bank, func=AF.Copy,
                     reads=[pt("b%d" % pb)], writes=[tpr])

            def conv_s1b(ch, slot):
                pr, ac = pre[slot % 4], acc[slot % 6]
                tpr, tac = tl("pre%d" % (slot % 4)), tl("acc%d" % (slot % 6))
                P.op("act", "activation", out=ac[:], in_=pr[:, 0:TT], func=AF.Identity,
                     scale=cv[:, COL["convw"] + ch:COL["convw"] + ch + 1],
                     bias=cv[:, COL["convb"] + ch:COL["convb"] + ch + 1],
                     reads=[tpr, tc_], writes=[tac])

            def conv_tap(ch, slot, k):
                pr, ac = pre[slot % 4], acc[slot % 6]
                tpr, tac = tl("pre%d" % (slot % 4)), tl("acc%d" % (slot % 6))
                P.op("dve", "scalar_tensor_tensor", out=ac[:], in0=pr[:, k:k + TT],
                     scalar=cv[:, COL["convw"] + k * 32 + ch:COL["convw"] + k * 32 + ch + 1],
                     in1=ac[:], op0=ALU.mult, op1=ALU.add,
                     reads=[tpr, tac, tc_], writes=[tac])

            def conv_hist(ch, slot):
                pr = pre[slot % 4]
                P.op("dve", "tensor_copy", out=hist[:, ch, :], in_=pr[:, TT:TT + 3],
                     reads=[tl("pre%d" % (slot % 4))], writes=[tl("hist%d" % ch)])

            def conv_s3(ch, slot):
                ac, tac = acc[slot % 6], tl("acc%d" % (slot % 6))
                if ch < 16:
                    dst, dtile = xsT[:, ch, :], tl("xsT")
                elif ch < 24:
                    dst, dtile = BT[:, ch - 16, :], tl("BT")
                else:
                    dst, dtile = CT[:, ch - 24, :], tl("CT")
                P.op("act", "activation", out=dst, in_=ac[:], func=AF.Silu,
                     reads=[tac], writes=[dtile])

            def z_chunk(fc, slot, wz):
                grp, f = fc // 4, fc % 4
                if f == 0:
                    c0 = grp * 512
                    wz[grp] = self.wload(
                        self.d_win[:, c0:c0 + 512].rearrange("(kc p) n -> p kc n", p=128), (DC, 512))
                w_sb, t_w = wz[grp]
                pb = 5
                bank = ps[pb][:, 0:TT]
                for c in range(DC):
                    P.op("pe", "matmul", bank, w_sb[:, c, f * 128:(f + 1) * 128], h_t[:, c, :],
                         start=(c == 0), stop=(c == DC - 1), reads=[t_w, tl("h_t")],
                         writes=[pt("b%d" % pb)])
                P.op("act", "activation", out=zsT[:, fc, :], in_=bank, func=AF.Silu,
                     reads=[pt("b%d" % pb)], writes=[tl("zsT")])

            def conv_phase():
                wsb = {}
                wz = {}
                npair = 16
                for p_ in range(npair + 2):
                    if p_ < npair:
                        z_chunk(p_, 2 * p_ + 1, wz)
                    if p_ < npair:
                        chs = (2 * p_, 2 * p_ + 1)
                        for ch in chs:
                            grp = 4 + ch // 4
                            if ch % 4 == 0:
                                c0 = grp * 512
                                wsb[grp] = self.wload(
                                    self.d_win[:, c0:c0 + 512].rearrange("(kc p) n -> p kc n", p=128),
                                    (DC, 512))
                            conv_s1(ch, ch, wsb[grp][0], wsb[grp][1], ch % 4)
                        for ch in chs:
                            conv_s1b(ch, ch)
                    if 1 <= p_ <= npair:
                        chs = (2 * (p_ - 1), 2 * (p_ - 1) + 1)
                        for k in (1, 2, 3):
                            for ch in chs:
                                conv_tap(ch, ch, k)
                        for ch in chs:
                            conv_hist(ch, ch)
                    if p_ >= 2:
                        for ch in (2 * (p_ - 2), 2 * (p_ - 2) + 1):
                            conv_s3(ch, ch)

            def z_phase(itc):
                for grp in range(4):
                    c0 = grp * 512
                    w_sb, t_w = self.wload(
                        self.d_win[:, c0:c0 + 512].rearrange("(kc p) n -> p kc n", p=128), (DC, 512))
                    for f in range(4):
                        pb = pbanks[itc[0] % 4]
                        itc[0] += 1
                        bank = ps[pb][:, 0:TT]
                        for c in range(DC):
                            P.op("pe", "matmul", bank, w_sb[:, c, f * 128:(f + 1) * 128], h_t[:, c, :],
                                 start=(c == 0), stop=(c == DC - 1), reads=[t_w, tl("h_t")],
                                 writes=[pt("b%d" % pb)])
                        fc = grp * 4 + f
                        P.op("act", "activation", out=zsT[:, fc, :], in_=bank, func=AF.Silu,
                             reads=[pt("b%d" % pb)], writes=[tl("zsT")])

            for tt in range(NTT):
                t0 = tt * TT
                tsl = slice(t0, t0 + TT)
                xtile = [self.t_x[c][t0 // 256] for c in range(DC)]
                for c in range(DC):
                    k = c % 2
                    P.op("act", "activation", out=sqs[k][:], in_=self.xT[:, c, tsl], func=AF.Square,
                         reads=[xtile[c]], writes=[tl("sqs%d" % k)])
                    P.op("pe", "matmul", ps[2][:, 0:TT], self.ones_d[:], sqs[k][:],
                         start=(c == 0), stop=(c == DC - 1), reads=[tl("sqs%d" % k), tc_], writes=[pt("b2")])
                self.rstd_from_ms(rst[:], ps[2][:, 0:TT], [pt("b2")], tl("rst"))
                for c in range(DC):
                    P.op("dve", "scalar_tensor_tensor", out=h_t[:, c, :], in0=self.xT[:, c, tsl],
                         scalar=cv[:, gcol + c:gcol + c + 1], in1=rst[:], op0=ALU.mult, op1=ALU.mult,
                         reads=[xtile[c], tl("rst"), tc_], writes=[tl("h_t")])
                wdt, t_wdt = self.wload(
                    self.d_win[:, 6144:6176].rearrange("(kc p) n -> p kc n", p=128), (DC, 32))
                for j in range(2):
                    js = slice(j * 128, (j + 1) * 128)
                    dcol = slice(256 + 32 * j, 288 + 32 * j)
                    for c in range(DC):
                        P.op("pe", "matmul", ps[2][:, dcol], h_t[:, c, js], wdt[:, c, :],
                             start=(c == 0), stop=(c == DC - 1), reads=[tl("h_t"), t_wdt],
                             writes=[pt("b2")])
                for j in range(2):
                    dcol = slice(256 + 32 * j, 288 + 32 * j)
                    P.op("dve", "tensor_tensor", out=dtx[j][:], in0=ps[2][:, dcol],
                         in1=cv[:, COL["dtbias"]:COL["dtbias"] + 32], op=ALU.add,
                         reads=[pt("b2"), tc_], writes=[tl("dtx%d" % j)])
                for j in range(2):
                    P.op("act", "activation", out=dtx[j][:], in_=dtx[j][:], func=AF.Exp,
                         reads=[tl("dtx%d" % j)], writes=[tl("dtx%d" % j)])
                for j in range(2):
                    P.op("act", "activation", out=dts[j][:], in_=dtx[j][:], func=AF.Ln, bias=onec,
                         reads=[tl("dtx%d" % j), tc_], writes=[tl("dts%d" % j)])
                for j in range(2):
                    P.op("dve", "scalar_tensor_tensor", out=adt[j][:], in0=dts[j][:], scalar=-1.0,
                         in1=ealog[:], op0=ALU.mult, op1=ALU.mult,
                         reads=[tl("dts%d" % j), tl("ealog")], writes=[tl("adt%d" % j)])
                itc = [0]
                conv_phase()
                for j in range(2):
                    P.op("pe", "matmul", ps[2][:, 320 + 64 * j:352 + 64 * j], U_f, adt[j][:],
                         start=True, stop=True, reads=[tl("adt%d" % j), tc_], writes=[pt("b2")])
                    P.op("pe", "matmul", ps[2][:, 352 + 64 * j:384 + 64 * j], ones_f, adt[j][:],
                         start=True, stop=True, reads=[tl("adt%d" % j), tc_], writes=[pt("b2")])
                for j in range(2):
                    a_ps = ps[2][:, 320 + 64 * j:352 + 64 * j]
                    t_ps_ = ps[2][:, 352 + 64 * j:384 + 64 * j]
                    P.op("dve", "tensor_copy", out=acs[j][:], in_=a_ps,
                         reads=[pt("b2")], writes=[tl("acs%d" % j)])
                    P.op("dve", "tensor_tensor", out=t32[j][:], in0=t_ps_, in1=acs[j][:],
                         op=ALU.subtract, reads=[pt("b2"), tl("acs%d" % j)], writes=[tl("t32%d" % j)])
                    P.op("act", "activation", out=cdc[j][:], in_=t_ps_, func=AF.Exp,
                         reads=[pt("b2")], writes=[tl("cdc%d" % j)])
                for j in range(2):
                    P.op("act", "activation", out=t32[j][:], in_=t32[j][:], func=AF.Exp,
                         reads=[tl("t32%d" % j)], writes=[tl("t32%d" % j)])
                for j in range(2):
                    P.op("dve", "tensor_tensor", out=wgt[j][:], in0=t32[j][:], in1=dts[j][:], op=ALU.mult,
                         reads=[tl("t32%d" % j), tl("dts%d" % j)], writes=[tl("wgt%d" % j)])
                wo_pieces = {}
                for chf in range(2):
                    for rh in range(2):
                        wo_pieces[(chf, rh)] = self.wload(
                            self.d_wout[rh * 1024:(rh + 1) * 1024, chf * 512:(chf + 1) * 512].rearrange(
                                "(a p) n -> p a n", p=128), (8, 512))
                for j in range(2):
                    js = slice(j * 128, (j + 1) * 128)
                    for q4 in range(4):
                        for f in range(4):
                            fc = q4 * 4 + f
                            P.op("pe", "matmul", ps[3][:, f * 128:(f + 1) * 128], xsT[:, fc, js], ident_b,
                                 start=True, stop=True, reads=[tl("xsT"), tc_], writes=[pt("b3")])
                        P.op("act", "activation", out=x_tok[:, q4 * 512:(q4 + 1) * 512], in_=ps[3][:],
                             func=AF.Copy, reads=[pt("b3")], writes=[tl("x_tok")])
                    for q4 in range(2):
                        for f in range(4):
                            g = q4 * 4 + f
                            P.op("pe", "matmul", ps[3][:, f * 128:(f + 1) * 128], BT[:, g, js], ident_b,
                                 start=True, stop=True, reads=[tl("BT"), tc_], writes=[pt("b3")])
                        P.op("act", "activation", out=B_tok[:, q4 * 512:(q4 + 1) * 512], in_=ps[3][:],
                             func=AF.Copy, reads=[pt("b3")], writes=[tl("B_tok")])
                    P.op("dve", "tensor_tensor",
                         out=xdtd[:, :].rearrange("p (h d) -> p h d", h=32),
                         in0=x_tok[:, :].rearrange("p (h d) -> p h d", h=32),
                         in1=wgt[j][:, :].unsqueeze(2).to_broadcast([128, 32, 64]), op=ALU.mult,
                         reads=[tl("x_tok"), tl("wgt%d" % j)], writes=[tl("xdtd")])

                    def f1(g):
                        gb3 = g % 3
                        abk = 7 if g % 2 == 0 else 5
                        P.op("pe", "matmul", ps[4][:, 0:128], BT[:, g, js], CT[:, g, js],
                             start=True, stop=True, reads=[tl("BT"), tl("CT")], writes=[pt("b4")])
                        P.op("dve", "tensor_tensor", out=CBm[gb3][:], in0=ps[4][:, 0:128], in1=U_f, op=ALU.mult,
                             reads=[pt("b4"), tc_], writes=[tl("CBm%d" % gb3)])
                        for r in range(4):
                            hh = 4 * g + r
                            P.op("pe", "matmul", ps[abk][:, r * 128:(r + 1) * 128],
                                 adt[j][:, hh:hh + 1].to_broadcast([128, 128]), U_f,
                                 start=True, stop=True, reads=[tl("adt%d" % j), tc_], writes=[pt("b%d" % abk)])

                    def f2a(g):
                        gb = g % 2
                        abk = 7 if g % 2 == 0 else 5
                        for r in range(4):
                            hh = 4 * g + r
                            P.op("dve", "tensor_scalar", out=seg[:, r * 128:(r + 1) * 128],
                                 in0=ps[abk][:, r * 128:(r + 1) * 128], scalar1=acs[j][:, hh:hh + 1],
                                 scalar2=0.0, op0=ALU.subtract, op1=ALU.min,
                                 reads=[pt("b%d" % abk), tl("acs%d" % j)], writes=[tl("seg")])
                        P.op("act", "activation", out=Lh[:], in_=seg[:], func=AF.Exp,
                             reads=[tl("seg")], writes=[tl("Lh")])
                        P.op("act", "activation", out=eAB[:], in_=ps[abk][:], func=AF.Exp,
                             reads=[pt("b%d" % abk)], writes=[tl("eAB")])
                        P.op("dve", "tensor_tensor",
                             out=Cs[gb][:, :].rearrange("p (r l) -> p r l", r=4),
                             in0=eAB[:, :].rearrange("p (r l) -> p r l", r=4),
                             in1=CT[:, g, js].unsqueeze(1).to_broadcast([128, 4, 128]), op=ALU.mult,
                             reads=[tl("eAB"), tl("CT")], writes=[tl("Cs%d" % gb)])

                    def f2b(g):
                        gb = g % 2
                        gb3 = g % 3
                        for r in range(4):
                            hh = 4 * g + r
                            P.op("dve", "scalar_tensor_tensor", out=Mh[gb][:, r * 128:(r + 1) * 128],
                                 in0=Lh[:, r * 128:(r + 1) * 128], scalar=dts[j][:, hh:hh + 1], in1=CBm[gb3][:],
                                 op0=ALU.mult, op1=ALU.mult,
                                 reads=[tl("Lh"), tl("dts%d" % j), tl("CBm%d" % gb3)], writes=[tl("Mh%d" % gb)])

                    def back_a(g):
                        gb = g % 2
                        for r in range(4):
                            hh = 4 * g + r
                            half = (hh % 2) * 64
                            ybank = (hh // 2) % 2
                            yreg = ps[ybank][half:half + 64, 0:128]
                            ytile = pt("b%d" % ybank)
                            P.op("pe", "matmul", yreg, x_tok[:, hh * 64:(hh + 1) * 64],
                                 Mh[gb][:, r * 128:(r + 1) * 128], start=True, stop=False,
                                 tile_position=(0, half),
                                 reads=[tl("x_tok"), tl("Mh%d" % gb)], writes=[ytile])
                            P.op("pe", "matmul", yreg, prevT[:, hh * 64:(hh + 1) * 64],
                                 Cs[gb][:, r * 128:(r + 1) * 128], start=False, stop=True,
                                 tile_position=(0, half),
                                 reads=[tl("prevT%d" % g), tl("Cs%d" % gb)], writes=[ytile])
                            if hh % 2 == 1:
                                fc = hh // 2
                                fi = fc % 2
                                ygt = tl("yg%d_%d" % (gb, fi))
                                yps = ps[ybank][:, 0:128]
                                P.op("dve", "scalar_tensor_tensor", out=yg[gb][:, fi, :], in0=xsT[:, fc, js],
                                     scalar=cv[:, COL["dskip"] + fc:COL["dskip"] + fc + 1], in1=yps,
                                     op0=ALU.mult, op1=ALU.add,
                                     reads=[tl("xsT"), ytile, tc_], writes=[ygt])
                        gs_ = slice(g * 256, (g + 1) * 256)
                        P.op("pe", "matmul", ps[6][:, 0:256], B_tok[:, g * 128:(g + 1) * 128], xdtd[:, gs_],
                             start=True, stop=True, reads=[tl("B_tok"), tl("xdtd")], writes=[pt("b6")])
                        for fi in (0, 1):
                            fc = 2 * g + fi
                            ygt = tl("yg%d_%d" % (gb, fi))
                            P.op("dve", "tensor_tensor", out=yg[gb][:, fi, :], in0=yg[gb][:, fi, :],
                                 in1=zsT[:, fc, js], op=ALU.mult,
                                 reads=[ygt, tl("zsT")], writes=[ygt])
                        for fi in (0, 1):
                            ygt = tl("yg%d_%d" % (gb, fi))
                            P.op("act", "activation", out=ysq[fi][:], in_=yg[gb][:, fi, :], func=AF.Square,
                                 reads=[ygt], writes=[tl("ysq%d" % fi)])
                            P.op("pe", "matmul", ps[2][:, 0:128], ones_g[:], ysq[fi][:],
                                 start=(fi == 0), stop=(fi == 1),
                                 reads=[tl("ysq%d" % fi), tl("ones_g")], writes=[pt("b2")])
                        self.rstd_from_ms(rsg[gb][:], ps[2][:, 0:128], [pt("b2")], tl("rsg%d" % gb))

                    def back_b(g):
                        gb = g % 2
                        gs_ = slice(g * 256, (g + 1) * 256)
                        P.op("dve", "tensor_tensor",
                             out=stmp[:, :].rearrange("p (r d) -> p r d", r=4),
                             in0=S[:, gs_].rearrange("p (r d) -> p r d", r=4),
                             in1=cdc[j][:, 4 * g:4 * g + 4].unsqueeze(2).to_broadcast([128, 4, 64]),
                             op=ALU.mult, reads=[tl("S%d" % g), tl("cdc%d" % j)], writes=[tl("stmp")])
                        P.op("dve", "tensor_tensor", out=S[:, gs_], in0=stmp[:], in1=ps[6][:, 0:256],
                             op=ALU.add, reads=[tl("stmp"), pt("b6")], writes=[tl("S%d" % g)])
                        P.op("act", "activation", out=prevT[:, gs_], in_=S[:, gs_], func=AF.Copy,
                             reads=[tl("S%d" % g)], writes=[tl("prevT%d" % g)])
                        for fi in (0, 1):
                            fc = 2 * g + fi
                            P.op("dve", "scalar_tensor_tensor", out=ynT[:, fc, js], in0=yg[gb][:, fi, :],
                                 scalar=cv[:, COL["ssm_norm"] + fc:COL["ssm_norm"] + fc + 1], in1=rsg[gb][:],
                                 op0=ALU.mult, op1=ALU.mult,
                                 reads=[tl("yg%d_%d" % (gb, fi)), tl("rsg%d" % gb), tc_], writes=[tl("ynT")])

                    for step in range(-2, 8):
                        if 0 <= step + 2 < 8:
                            f1(step + 2)
                        if 0 <= step + 1 < 8:
                            f2a(step + 1)
                        if step >= 0:
                            back_a(step)
                        if 0 <= step + 1 < 8:
                            f2b(step + 1)
                        if step >= 0:
                            back_b(step)
                obanks = [3, 4, 6, 7]
                for chf in range(2):
                    for rh in range(2):
                        wo_sb, t_wo = wo_pieces[(chf, rh)]
                        for d4 in range(4):
                            bk = obanks[d4]
                            for f in range(8):
                                fc = rh * 8 + f
                                P.op("pe", "matmul", ps[bk][:, 0:TT], wo_sb[:, f, d4 * 128:(d4 + 1) * 128],
                                     ynT[:, fc, :], start=(rh == 0 and f == 0), stop=(rh == 1 and f == 7),
                                     reads=[t_wo, tl("ynT")], writes=[pt("b%d" % bk)])
                    for d4 in range(4):
                        dc = chf * 4 + d4
                        bk = obanks[d4]
                        P.op("dve", "tensor_tensor", out=self.xT[:, dc, tsl], in0=ps[bk][:, 0:TT],
                             in1=self.xT[:, dc, tsl], op=ALU.add,
                             reads=[pt("b%d" % bk), xtile[dc]], writes=[xtile[dc]])


def make_cvec(inp):
    cv = np.zeros((128, NCOL), np.float32)
    for l in range(2):
        for n in ("ffn1_norm", "mix_norm", "ffn2_norm"):
            c0 = COL[(n, l)]
            cv[:, c0:c0 + 8] = np.asarray(inp[n][l], np.float32).reshape(8, 128).T
    cv[:, COL["q_norm"]] = np.asarray(inp["attn_q_norm"][0], np.float32)
    cv[:, COL["k_norm"]] = np.asarray(inp["attn_k_norm"][0], np.float32)
    half = HD // 2
    invf = (10000.0 ** (-np.arange(half, dtype=np.float32) / half)).astype(np.float32)
    cv[:, COL["inv_freq"]] = np.concatenate([invf, invf])
    cv[:, COL["sgn"]] = np.concatenate([-np.ones(half), np.ones(half)]).astype(np.float32)
    cv[:, COL["ident"]:COL["ident"] + 128] = np.eye(128, dtype=np.float32)
    cw = np.asarray(inp["ssm_conv_w"][0], np.float32)
    for k in range(4):
        cv[:, COL["convw"] + k * 32:COL["convw"] + (k + 1) * 32] = cw[k].reshape(32, 128).T
    cv[:, COL["convb"]:COL["convb"] + 32] = np.asarray(inp["ssm_conv_b"][0], np.float32).reshape(32, 128).T
    cv[:, COL["ssm_norm"]:COL["ssm_norm"] + 16] = np.asarray(inp["ssm_norm"][0], np.float32).reshape(16, 128).T
    cv[:, COL["dskip"]:COL["dskip"] + 16] = np.repeat(
        np.asarray(inp["ssm_d"][0], np.float32), 64).reshape(16, 128).T
    cv[:, COL["dtbias"]:COL["dtbias"] + 32] = np.asarray(inp["ssm_dt_bias"][0], np.float32)[None, :]
    cv[:, COL["alog"]:COL["alog"] + 32] = np.asarray(inp["ssm_a_log"][0], np.float32)[None, :]
    cv[:, COL["U"]:COL["U"] + 128] = np.triu(np.ones((128, 128), np.float32))
    cv[:, COL["onesf"]:COL["onesf"] + 128] = 1.0
    cv[:, COL["one"]] = 1.0
    pb = np.zeros((16, 8), np.float32)
    for qc in range(16):
        pb[qc, (qc // 2):] = NEG
    cv[:, COL["pastbias"]:COL["pastbias"] + 128] = pb.reshape(1, 128)
    return cv


def make_cbf():
    cbm = np.zeros((128, NCB), np.float32)
    cbm[:, CB["ident"]:CB["ident"] + 128] = np.eye(128)
    sw = np.zeros((128, 128), np.float32)
    for m in range(128):
        sw[(m + 64) % 128, m] = 1.0
    cbm[:, CB["swap"]:CB["swap"] + 128] = sw
    kk = np.arange(128)[:, None]
    qq = np.arange(256)[None, :]
    for kc in range(2):
        cbm[:, CB["causal"] + kc * 256:CB["causal"] + (kc + 1) * 256] = np.where(
            kc * 128 + kk <= qq, 0.0, NEGM)
    for n in range(8):
        cbm[n, CB["onehot"] + n * 128:CB["onehot"] + (n + 1) * 128] = 1.0
    cbm[:, CB["ones"]:CB["ones"] + 128] = 1.0
    return cbm


def make_in_maps(inp, cfg, ncores):
    f = lambda a: np.ascontiguousarray(np.asarray(a, dtype=np.float32))
    shared = {
        "cvec": make_cvec(inp),
        "wg1": f(inp["ffn1_w_gate"]), "wu1": f(inp["ffn1_w_up"]), "wd1": f(inp["ffn1_w_down"]),
        "wg2": f(inp["ffn2_w_gate"]), "wu2": f(inp["ffn2_w_up"]), "wd2": f(inp["ffn2_w_down"]),
    }
    wqkv = f(inp["attn_w_qkv"])[0]
    shared["wqkv"] = np.ascontiguousarray(
        wqkv.reshape(D, 3, NH, HD).transpose(2, 1, 0, 3).reshape(NH, 3 * D, HD))
    shared["wo"] = f(inp["attn_w_o"])[0]
    shared["cbf"] = make_cbf()
    shared["w_in"] = f(inp["ssm_w_in"])[0]
    shared["w_out"] = f(inp["ssm_w_out"])[0]
    pos = np.asarray(inp["positions"]).astype(np.int32)
    x = np.asarray(inp["x"], np.float32)
    maps = []
    for i in range(ncores):
        xs = x[i * cfg.nseq:(i + 1) * cfg.nseq, :cfg.T]
        m = dict(shared)
        m["xT"] = np.ascontiguousarray(xs.transpose(0, 2, 1))
        ps_ = pos[i * cfg.nseq:(i + 1) * cfg.nseq, :cfg.T]
        m["pos"] = np.ascontiguousarray(np.broadcast_to(ps_[:, None, :], (cfg.nseq, 128, cfg.T)))
        maps.append(m)
    return maps


_NC_CACHE = {}


def run(inp, cfg, ncores, trace=False):
    key = (cfg.nseq, cfg.T, tuple(cfg.phases or ()))
    if key not in _NC_CACHE:
        _NC_CACHE[key] = Builder(cfg).build()
    nc = _NC_CACHE[key]
    maps = make_in_maps(inp, cfg, ncores)
    res = run_bass_kernel_spmd(nc, maps, core_ids=list(range(ncores)), trace=trace)
    outs = [np.asarray(r["outT"]).transpose(0, 2, 1) for r in res.results]
    return np.concatenate(outs, axis=0), res


def kernel(**inputs):
    cfg = Cfg(nseq=2, T=2048)
    out, _ = run(inputs, cfg, NCORES)
    return np.ascontiguousarray(out.astype(np.float32))
```
